# Optimizing a Trainium2 kernel written in Bass

```python
import math
import jax, jax.numpy as jnp
from jax import lax
import numpy as np

D_MODEL = 2048
BATCH = 8
SEQ = 2048
DEPTH = 1

CHUNK = 64
Q_BLOCK = 128
D_MIX = D_MODEL
CONV_CH = D_MIX // 2
CONV_K = 31
ATT_HEADS = 8
ATT_QK_DIM = 64
ATT_V_DIM = 2 * ATT_QK_DIM
ATT_QK = ATT_HEADS * 2 * ATT_QK_DIM
ATT_V = ATT_HEADS * ATT_V_DIM
D_IN = 2 * CONV_CH + 2 * ATT_QK + ATT_V
D_FF = 5632
FFN_CONV_K = 3
NORM_EPS = 1e-6
LN_EPS = 1e-5

kernel_name = "hybrid_conformer_conv_diff_attn_block"


def rmsnorm(x, g):
    xf = x.astype(jnp.float32)
    y = xf * lax.rsqrt(jnp.mean(xf * xf, axis=-1, keepdims=True) + NORM_EPS)
    return (y * g.astype(jnp.float32)).astype(x.dtype)


def layernorm(x, g, b):
    xf = x.astype(jnp.float32)
    mu = jnp.mean(xf, axis=-1, keepdims=True)
    var = jnp.mean(jnp.square(xf - mu), axis=-1, keepdims=True)
    y = (xf - mu) * lax.rsqrt(var + LN_EPS)
    return (y * g.astype(jnp.float32) + b.astype(jnp.float32)).astype(x.dtype)


def causal_dwconv(x, w, b):
    k = w.shape[0]
    y = lax.conv_general_dilated(
        x, w.astype(x.dtype)[:, None, :], window_strides=(1,),
        padding=[(k - 1, 0)], dimension_numbers=("NWC", "WIO", "NWC"),
        feature_group_count=x.shape[-1])
    return y + b.astype(x.dtype)


def diff_attention(q, k, v, lam):
    seq = q.shape[1]
    scale = 1.0 / math.sqrt(ATT_QK_DIM)
    outs = []
    for i in range(seq // Q_BLOCK):
        q0 = i * Q_BLOCK
        kv_len = q0 + Q_BLOCK
        qs = q[:, q0:kv_len]
        ks = k[:, :kv_len]
        vs = v[:, :kv_len]
        s = jnp.einsum("bqhmd,bkhmd->bhmqk", qs, ks,
                       preferred_element_type=jnp.float32) * scale
        q_chunk = (q0 + jnp.arange(Q_BLOCK)) // CHUNK
        k_chunk = jnp.arange(kv_len) // CHUNK
        allowed = k_chunk[None, :] <= q_chunk[:, None]
        s = jnp.where(allowed, s, jnp.float32(-1e30))
        p = jax.nn.softmax(s, axis=-1)
        a = p[:, :, 0] - lam * p[:, :, 1]
        outs.append(jnp.einsum("bhqk,bkhd->bqhd", a.astype(v.dtype), vs))
    return jnp.concatenate(outs, axis=1)


def hybrid_layer(x, layer_idx, norm_mix_g, w_in, conv_dw_w, conv_dw_b, conv_ln_g,
                 conv_ln_b, lambda_q1, lambda_k1, lambda_q2, lambda_k2, subln_g,
                 w_out, norm_ffn_g, w_up, ffn_dw_w, ffn_dw_b, w_down):
    bsz, seq, _ = x.shape
    h = rmsnorm(x, norm_mix_g)
    z = h @ w_in
    a, g, q, k, v = jnp.split(
        z, [CONV_CH, 2 * CONV_CH, 2 * CONV_CH + ATT_QK, 2 * CONV_CH + 2 * ATT_QK], axis=-1)

    u = a * jax.nn.sigmoid(g)
    u = causal_dwconv(u, conv_dw_w, conv_dw_b)
    u = layernorm(u, conv_ln_g, conv_ln_b)
    conv_out = jax.nn.silu(u)

    lambda_init = 0.8 - 0.6 * math.exp(-0.3 * layer_idx)
    lam = (jnp.exp(jnp.sum(lambda_q1.astype(jnp.float32) * lambda_k1.astype(jnp.float32)))
           - jnp.exp(jnp.sum(lambda_q2.astype(jnp.float32) * lambda_k2.astype(jnp.float32)))
           + lambda_init)
    q = q.reshape(bsz, seq, ATT_HEADS, 2, ATT_QK_DIM)
    k = k.reshape(bsz, seq, ATT_HEADS, 2, ATT_QK_DIM)
    v = v.reshape(bsz, seq, ATT_HEADS, ATT_V_DIM)
    o = diff_attention(q, k, v, lam)
    o = rmsnorm(o, subln_g) * (1.0 - lambda_init)
    attn_out = o.reshape(bsz, seq, ATT_V)

    mix = jnp.concatenate([conv_out, attn_out], axis=-1) @ w_out
    x = x + mix

    h = rmsnorm(x, norm_ffn_g)
    gate, up = jnp.split(h @ w_up, 2, axis=-1)
    gate = causal_dwconv(gate, ffn_dw_w, ffn_dw_b)
    x = x + (jax.nn.gelu(gate, approximate=False) * up) @ w_down
    return x


def setup_inputs(seed: int = 0) -> dict:
    key = jax.random.key(seed)
    ks = jax.random.split(key, 20)
    f32 = jnp.float32
    L = DEPTH

    def nrm(k, shape, scale):
        return jax.random.normal(k, shape, f32) * scale

    return {
        "x": nrm(ks[0], (BATCH, SEQ, D_MODEL), 1.0),
        "norm_mix_g": 1.0 + nrm(ks[1], (L, D_MODEL), 0.02),
        "w_in": nrm(ks[2], (L, D_MODEL, D_IN), D_MODEL ** -0.5),
        "conv_dw_w": nrm(ks[3], (L, CONV_K, CONV_CH), CONV_K ** -0.5),
        "conv_dw_b": nrm(ks[4], (L, CONV_CH), 0.02),
        "conv_ln_g": 1.0 + nrm(ks[5], (L, CONV_CH), 0.02),
        "conv_ln_b": nrm(ks[6], (L, CONV_CH), 0.02),
        "lambda_q1": nrm(ks[7], (L, ATT_QK_DIM), 0.1),
        "lambda_k1": nrm(ks[8], (L, ATT_QK_DIM), 0.1),
        "lambda_q2": nrm(ks[9], (L, ATT_QK_DIM), 0.1),
        "lambda_k2": nrm(ks[10], (L, ATT_QK_DIM), 0.1),
        "subln_g": 1.0 + nrm(ks[11], (L, ATT_V_DIM), 0.02),
        "w_out": nrm(ks[12], (L, D_MIX, D_MODEL), D_MIX ** -0.5),
        "norm_ffn_g": 1.0 + nrm(ks[13], (L, D_MODEL), 0.02),
        "w_up": nrm(ks[14], (L, D_MODEL, 2 * D_FF), D_MODEL ** -0.5),
        "ffn_dw_w": nrm(ks[15], (L, FFN_CONV_K, D_FF), FFN_CONV_K ** -0.5),
        "ffn_dw_b": nrm(ks[16], (L, D_FF), 0.02),
        "w_down": nrm(ks[17], (L, D_FF, D_MODEL), D_FF ** -0.5),
        "norm_final_g": 1.0 + nrm(ks[18], (D_MODEL,), 0.02),
    }


def reference(x, norm_mix_g, w_in, conv_dw_w, conv_dw_b, conv_ln_g, conv_ln_b,
              lambda_q1, lambda_k1, lambda_q2, lambda_k2, subln_g, w_out,
              norm_ffn_g, w_up, ffn_dw_w, ffn_dw_b, w_down, norm_final_g):
    for l in range(DEPTH):
        x = hybrid_layer(x, l, norm_mix_g[l], w_in[l], conv_dw_w[l], conv_dw_b[l],
                         conv_ln_g[l], conv_ln_b[l], lambda_q1[l], lambda_k1[l],
                         lambda_q2[l], lambda_k2[l], subln_g[l], w_out[l],
                         norm_ffn_g[l], w_up[l], ffn_dw_w[l], ffn_dw_b[l], w_down[l])
    return rmsnorm(x, norm_final_g)
```

```python
import numpy as np
import concourse.bass as bass
import concourse.mybir as mybir
from concourse.bass_utils import run_bass_kernel_spmd

F32 = mybir.dt.float32
BF16 = mybir.dt.bfloat16
AF = mybir.ActivationFunctionType
ALU = mybir.AluOpType

D = 2048
SEQ = 2048
NCORES = 8
DIN = 5120
DFF = 5632
NG = 4
NFC = 44
NORM_EPS = 1e-6
LN_EPS = 1e-5
LAMBDA_INIT = 0.2

C_GMIX, C_GFFN, C_W31, C_CB, C_LNG, C_LNB = 0, 16, 32, 280, 288, 296
C_FW, C_FB, C_LQ1, C_LK1, C_LQ2, C_LK2, C_SUB, C_TOT = 304, 436, 480, 544, 608, 672, 736, 737


class Tracker:
    def __init__(self, nc):
        self.nc = nc
        self.ops = []
        self.lastw = {}
        self.readers = {}

    def add(self, eng, fn, reads=(), writes=(), dma=False):
        idx = len(self.ops)
        deps = set()
        for k in reads:
            w = self.lastw.get(k)
            if w is not None:
                deps.add(w)
        for k in writes:
            w = self.lastw.get(k)
            if w is not None:
                deps.add(w)
            for r in self.readers.get(k, {}).values():
                deps.add(r)
        for k in reads:
            rk = ("dma", idx) if dma else eng
            self.readers.setdefault(k, {})[rk] = idx
        for k in writes:
            self.lastw[k] = idx
            self.readers[k] = {}
        deps.discard(idx)
        self.ops.append(dict(eng=eng, fn=fn, deps=deps, dma=dma))
        return idx

    def inherit(self, new_keys, old_keys):
        for nk in new_keys:
            rd = self.readers.setdefault(nk, {})
            for ok in old_keys:
                w = self.lastw.get(ok)
                if w is not None:
                    rd[("al", ok, "w")] = w
                for rk, r in self.readers.get(ok, {}).items():
                    rd[("al", ok, rk)] = r

    def emit(self):
        nc = self.nc
        engs = {"pe": nc.tensor, "act": nc.scalar, "dve": nc.vector,
                "pool": nc.gpsimd, "sp": nc.sync}
        ops = self.ops
        n = len(ops)
        need = [False] * n
        for op in ops:
            for d in op["deps"]:
                need[d] = True
        esem = {e: nc.alloc_semaphore("es_" + e) for e in ("pe", "act", "dve", "pool")}
        NS = {"sp": 8, "pool": 8, "act": 4}
        dsem = {q: [nc.alloc_semaphore(f"ds_{q}_{i}") for i in range(NS[q])] for q in NS}
        ecount = {e: 0 for e in esem}
        dcount = {q: 0 for q in NS}
        dlast = {q: [None] * NS[q] for q in NS}
        eclock = {e: {} for e in engs}
        token = [None] * n
        opclock = [None] * n
        nwaits = 0
        for i, op in enumerate(ops):
            e = op["eng"]
            eng = engs[e]
            clk = eclock[e]
            cands = []
            for d in op["deps"]:
                if (not op["dma"]) and e == "pe" and ops[d]["eng"] == "pe" and not ops[d]["dma"]:
                    continue
                cands.append(d)
            my_tok = None
            if op["dma"]:
                j = dcount[e]
                slot = j % NS[e]
                prev = dlast[e][slot]
                if prev is not None:
                    cands.append(prev)
                my_tok = (("d", e, slot), 16 * (j // NS[e] + 1))
                dcount[e] += 1
                dlast[e][slot] = i
            cands.sort(key=lambda d: -token[d][1])
            waits = {}
            for d in cands:
                sk, val = token[d]
                if clk.get(sk, 0) >= val:
                    continue
                waits[sk] = max(waits.get(sk, 0), val)
                for k2, v2 in opclock[d].items():
                    if clk.get(k2, 0) < v2:
                        clk[k2] = v2
            wl = [(sk, v) for sk, v in waits.items()]

            def semof(sk):
                return esem[sk[1]] if sk[0] == "e" else dsem[sk[1]][sk[2]]

            for sk, v in wl[1:]:
                eng.wait_ge(semof(sk), v)
                nwaits += 1
            r = op["fn"]()
            if isinstance(r, tuple):
                first, last = r
            else:
                first = last = r
            if wl:
                first._wait_ge(semof(wl[0][0]), wl[0][1])
            if op["dma"]:
                last.then_inc(semof(my_tok[0]), 16)
                token[i] = my_tok
                oc = dict(clk)
                oc[my_tok[0]] = my_tok[1]
                opclock[i] = oc
            else:
                if need[i]:
                    ecount[e] += 1
                    last.then_inc(esem[e], 1)
                token[i] = (("e", e), ecount[e])
                oc = dict(clk)
                oc[("e", e)] = ecount[e]
                opclock[i] = oc
        self.final_tokens = (token, opclock)
        self.dsem = dsem
        self.esem = esem
        self.dlast_val = {q: [ (16 * ((dcount[q] - 1 - sl) // NS[q] + 1) if dcount[q] > sl else 0) for sl in range(NS[q])] for q in NS}
        return nwaits


def build(debug=False):
    nc = bass.Bass("TRN2", target_bir_lowering=False)
    T = Tracker(nc)

    x_d = nc.dram_tensor("x", [SEQ, D], F32, kind="ExternalInput").ap()
    win_d = nc.dram_tensor("w_in", [D, DIN], F32, kind="ExternalInput").ap()
    wout_d = nc.dram_tensor("w_out", [D, D], F32, kind="ExternalInput").ap()
    wup_d = nc.dram_tensor("w_up", [D, 2 * DFF], F32, kind="ExternalInput").ap()
    wdn_d = nc.dram_tensor("w_down", [DFF, D], F32, kind="ExternalInput").ap()
    cst_d = nc.dram_tensor("consts", [128, C_TOT], F32, kind="ExternalInput").ap()
    gfin_d = nc.dram_tensor("gfin", [128, D], F32, kind="ExternalInput").ap()
    out_d = nc.dram_tensor("out", [SEQ, D], F32, kind="ExternalOutput").ap()
    dbg = {}
    if debug:
        for name, shape in debug.items():
            dbg[name] = nc.dram_tensor("dbg_" + name, list(shape), F32, kind="ExternalOutput").ap()

    win_v = win_d.rearrange("(kt p) f -> p kt f", p=128)
    wout_v = wout_d.rearrange("(kt p) f -> p kt f", p=128)
    wup_v = wup_d.rearrange("(kt p) f -> p kt f", p=128)
    wdn_v = wdn_d.rearrange("(fc p) d -> p fc d", p=128)

    cst = nc.alloc_sbuf_tensor("cst", [128, C_TOT], F32)
    kT = nc.alloc_sbuf_tensor("kT", [128, 8, SEQ], BF16)
    Vaug = nc.alloc_sbuf_tensor("Vaug", [128, 16, 8, 129], BF16)
    xres = nc.alloc_sbuf_tensor("xres", [128, 4, D], F32)
    R = nc.alloc_sbuf_tensor("R", [128, 8192 + 4096 + 4336 + 4096], BF16)
    hT = R[:, 0:8192].rearrange("p (k t) -> p k t", t=512)
    qT = R[:, 8192:12288].rearrange("p (h t) -> p h t", t=512)
    uT = R[:, 12288:16624].rearrange("p (c t) -> p c t", t=542)
    attnT = R[:, 16624:20720].rearrange("p (c t) -> p c t", t=512)
    act = R[:, 8192:8192 + 22 * 512].rearrange("p (f t) -> p f t", t=512)
    PT = [R[:, b * 1024:(b + 1) * 1024].rearrange("p (m t) -> p m t", t=512) for b in range(4)]
    attn_tok = R[:, 4096:8192].rearrange("p (i f) -> p i f", f=1024)
    diag = [R[:, 16624:16624 + 3968].rearrange("p (j m) -> p j m", m=128),
            R[:, 0:3968].rearrange("p (j m) -> p j m", m=128)]
    wring = nc.alloc_sbuf_tensor("wring", [128, 4, 4096], BF16)
    SA = nc.alloc_sbuf_tensor("SA", [128, 3600], F32)
    gfin = SA[:, 0:2048]
    cbuf = [SA[:, 2064 + i * 512:2064 + (i + 1) * 512] for i in range(3)]
    osb = SA[:, 0:512].rearrange("p (i d) -> p i d", d=128)
    t0b4 = SA[:, 512:1024].rearrange("p (i d) -> p i d", d=128)
    Osb = SA[:, 1024:1024 + 1032].rearrange("p (b c) -> p b c", c=258)
    SB = nc.alloc_sbuf_tensor("SB", [128, 8192], BF16)
    hb = [SB[:, 0:2048], SB[:, 2048:4096]]
    sig = [SB[:, 4096:5120].bitcast(F32), SB[:, 5120:6144].bitcast(F32)]
    ybf = SB[:, 0:4096].rearrange("p (c t) -> p c t", t=512)
    ysq_all = SB[:, 4096:8192].rearrange("p (c t) -> p c t", t=512)
    stat = [SB[:, 4096 + i * 1024:4096 + (i + 1) * 1024].bitcast(F32) for i in range(3)]
    ident = nc.alloc_sbuf_tensor("ident", [128, 128], BF16)
    identf = nc.alloc_sbuf_tensor("identf", [128, 128], F32)
    ones = nc.alloc_sbuf_tensor("ones", [128, 128], BF16)
    mhalf = nc.alloc_sbuf_tensor("mhalf", [128, 8], F32)
    st = nc.alloc_sbuf_tensor("st", [128, 64], F32)
    lamt = nc.alloc_sbuf_tensor("lamt", [128, 8], F32)
    lj = nc.alloc_sbuf_tensor("lj", [128, 64], F32)
    gs = nc.alloc_sbuf_tensor("gs", [128, 1], F32)
    rec = nc.alloc_sbuf_tensor("rec", [128, 8], F32)
    halo_save = nc.alloc_sbuf_tensor("halo_save", [128, 8, 30], BF16)
    fhalo = nc.alloc_sbuf_tensor("fhalo", [128, NFC, 2], F32)

    ps = nc.alloc_psum_tensor("ps", [128, 8, 512], F32)
    psT16 = ps[:, 6:8, :].bitcast(BF16).rearrange("p b (k t) -> p (b k) t", t=128)
    psT8 = ps[:, 0:1, :].bitcast(BF16).rearrange("p b (k t) -> p (b k) t", t=128)

    V, A_, P_, PE, SP = nc.vector, nc.scalar, nc.gpsimd, nc.tensor, nc.sync

    def col(c):
        return cst[:, c:c + 1]

    slabs = []
    for tg in range(NG):
        for p in range(4):
            slabs.append((win_v[:, :, p * 256:(p + 1) * 256], "k16"))
            slabs.append((win_v[:, :, 1024 + p * 256:1024 + (p + 1) * 256], "k16"))
        for s in range(12):
            slabs.append((win_v[:, :, 2048 + s * 256:2048 + (s + 1) * 256], "k16"))
        for s in range(8):
            slabs.append((wout_v[:, :, s * 256:(s + 1) * 256], "k16"))
        for hh in range(2):
            for p in range(11):
                c0 = (hh * 22 + 2 * p) * 128
                slabs.append((wup_v[:, :, c0:c0 + 256], "k16"))
                slabs.append((wup_v[:, :, DFF + c0:DFF + c0 + 256], "k16"))
            for dq in range(4):
                for (f0, nf) in ((0, 8), (8, 8), (16, 6)):
                    slabs.append((wdn_v[:, hh * 22 + f0:hh * 22 + f0 + nf, dq * 512:(dq + 1) * 512], ("f", nf)))
    ws_state = {"issued": 0, "used": 0}

    def ws_issue(upto):
        while ws_state["issued"] < min(upto, len(slabs)):
            n_ = ws_state["issued"]
            src, kind = slabs[n_]
            slot = n_ % 4
            if kind == "k16":
                dst = wring[:, slot, :].rearrange("p (k c) -> p k c", c=256)
            else:
                dst = wring[:, slot, 0:kind[1] * 512].rearrange("p (f c) -> p f c", c=512)
            T.add("pool", (lambda dst=dst, src=src: P_.dma_start(out=dst, in_=src)),
                  writes=[("ws", slot)], dma=True)
            ws_state["issued"] += 1

    def ws_next(kind_expect, held=0):
        n_ = ws_state["used"]
        ws_state["used"] += 1
        ws_issue(n_ - held + 4)
        src, kind = slabs[n_]
        assert (kind == "k16") == (kind_expect == "k16"), (n_, kind, kind_expect)
        slot = n_ % 4
        if kind == "k16":
            view = wring[:, slot, :].rearrange("p (k c) -> p k c", c=256)
        else:
            view = wring[:, slot, 0:kind[1] * 512].rearrange("p (f c) -> p f c", c=512)
        return view, ("ws", slot)

    def mm_group(out_ap, pairs, reads, writes, start_first=True, stop_last=True):
        def fn():
            first = last = None
            nmm = len(pairs)
            for i_, (l_, r_) in enumerate(pairs):
                ins = PE.matmul(out_ap, l_, r_, start=(i_ == 0 and start_first), stop=(i_ == nmm - 1 and stop_last))
                if first is None:
                    first = ins
                last = ins
            return first, last
        T.add("pe", fn, reads=reads, writes=writes)

    def dbg_dump(name, src_ap, reads, view=None):
        if name in dbg:
            dst = dbg[name] if view is None else view(dbg[name])
            T.add("pool", (lambda: P_.dma_start(out=dst, in_=src_ap)), reads=reads, dma=True)

    def rms_stats(src, src_key, junk, junk_key, c0, inv_n, eps):
        T.add("act", lambda: A_.activation(out=junk, in_=src, func=AF.Square, accum_out=st[:, c0:c0 + 1]),
              reads=[src_key], writes=[junk_key, ("st", c0)])
        T.add("dve", lambda: V.tensor_scalar(out=st[:, c0 + 1:c0 + 2], in0=st[:, c0:c0 + 1], scalar1=inv_n,
                                             scalar2=eps, op0=ALU.mult, op1=ALU.add),
              reads=[("st", c0)], writes=[("st", c0 + 1)])
        T.add("pool", lambda: P_.tensor_tensor(out=st[:, c0 + 2:c0 + 3], in0=st[:, c0 + 1:c0 + 2],
                                               in1=mhalf[:, 0:1], op=ALU.pow),
              reads=[("st", c0 + 1), "mhalf"], writes=[("st", c0 + 2)])

    def norm_transpose(src, src_key, tt, gcol, dstT, dst_key):
        b = tt % 2
        c0 = 8 * tt
        rms_stats(src, src_key, hb[b], ("hb", b), c0, 1.0 / D, NORM_EPS)
        T.add("dve", lambda: V.tensor_scalar(out=hb[b], in0=src, scalar1=st[:, c0 + 2:c0 + 3], scalar2=None,
                                             op0=ALU.mult),
              reads=[src_key, ("st", c0 + 2)], writes=[("hb", b)])

        def tr():
            first = last = None
            for k in range(16):
                ins = PE.transpose(psT16[:, k, :], hb[b][:, k * 128:(k + 1) * 128], ident[:])
                if first is None:
                    first = ins
                last = ins
            return first, last
        T.add("pe", tr, reads=[("hb", b), "ident"], writes=[("ps", 6), ("ps", 7)])
        gT = cst[:, gcol:gcol + 16]
        T.add("dve", lambda: V.tensor_tensor(out=dstT[:, :, tt * 128:(tt + 1) * 128], in0=psT16,
                                             in1=gT.unsqueeze(2).to_broadcast([128, 16, 128]), op=ALU.mult),
              reads=[("ps", 6), ("ps", 7), "cst"], writes=[dst_key(tt)])

    T.add("sp", lambda: SP.dma_start(out=cst[:], in_=cst_d[:, :]), writes=["cst"], dma=True)
    T.add("pool", lambda: P_.memset(identf[:], 0.0), writes=["identf"])
    T.add("pool", lambda: P_.affine_select(out=identf[:], in_=identf[:], pattern=[[-1, 128]],
                                           compare_op=ALU.not_equal, fill=1.0, base=0, channel_multiplier=1),
          reads=["identf"], writes=["identf"])
    T.add("pool", lambda: P_.tensor_copy(out=ident[:], in_=identf[:]), reads=["identf"], writes=["ident"])
    T.add("pool", lambda: P_.memset(ones[:], 1.0), writes=["ones"])
    T.add("pool", lambda: P_.memset(mhalf[:], -0.5), writes=["mhalf"])
    T.add("pool", lambda: P_.memset(Vaug[:, :, :, 128:129], 1.0), writes=[("V", t) for t in range(16)])
    T.add("pool", lambda: P_.memset(uT[:, :, 0:30], 0.0), writes=["uT_halo"])
    T.add("pool", lambda: P_.memset(fhalo[:], 0.0), writes=["fhalo"])
    T.add("dve", lambda: V.scalar_tensor_tensor(out=lj[:], in0=cst[:, C_LQ1:C_LQ1 + 64], scalar=1.0,
                                                in1=cst[:, C_LK1:C_LK1 + 64], op0=ALU.mult, op1=ALU.mult,
                                                accum_out=lamt[:, 0:1]),
          reads=["cst"], writes=["lj", ("lam", 0)])
    T.add("dve", lambda: V.scalar_tensor_tensor(out=lj[:], in0=cst[:, C_LQ2:C_LQ2 + 64], scalar=1.0,
                                                in1=cst[:, C_LK2:C_LK2 + 64], op0=ALU.mult, op1=ALU.mult,
                                                accum_out=lamt[:, 1:2]),
          reads=["cst"], writes=["lj", ("lam", 1)])
    T.add("act", lambda: A_.activation(out=lamt[:, 2:4], in_=lamt[:, 0:2], func=AF.Exp),
          reads=[("lam", 0), ("lam", 1)], writes=[("lam", 2)])
    T.add("dve", lambda: V.tensor_tensor(out=lamt[:, 4:5], in0=lamt[:, 2:3], in1=lamt[:, 3:4], op=ALU.subtract),
          reads=[("lam", 2)], writes=[("lam", 4)])
    T.add("dve", lambda: V.tensor_scalar(out=lamt[:, 5:6], in0=lamt[:, 4:5], scalar1=LAMBDA_INIT, scalar2=-1.0,
                                         op0=ALU.add, op1=ALU.mult),
          reads=[("lam", 4)], writes=["neglam"])
    T.add("dve", lambda: V.tensor_scalar(out=gs[:], in0=cst[:, C_SUB:C_SUB + 1], scalar1=1.0 - LAMBDA_INIT,
                                         scalar2=None, op0=ALU.mult),
          reads=["cst"], writes=["gs"])

    bank_rr = [0]
    YK = [("ybf", c) for c in range(8)]
    SQK = [("ysq", c) for c in range(8)]
    STK = [("stat", i) for i in range(3)]
    HBK = [("hb", 0), ("hb", 1), ("sig", 0), ("sig", 1)]
    N_PE = [4, 3, 2, 1]

    def next_bank(nb=6):
        b = bank_rr[0] % nb
        bank_rr[0] += 1
        return b

    for tg in range(NG):
        T0 = tg * 512
        T.inherit(HBK, YK + SQK + STK)
        T.inherit([("hT", t) for t in range(4)], [("h2T", t) for t in range(4)])
        T.inherit([("qT", h) for h in range(8)] + [("uT", c_) for c_ in range(8)] + ["uT_halo"] + [("attnT", t) for t in range(4)], ["act"])

        if tg == 0:
            for tt in range(4):
                r0 = T0 + tt * 128
                T.add("sp", (lambda tt=tt, r0=r0: SP.dma_start(out=xres[:, tt, :], in_=x_d[r0:r0 + 128, :])),
                      writes=[("xres", tt)], dma=True)
        for tt in range(4):
            norm_transpose(xres[:, tt, :], ("xres", tt), tt, C_GMIX, hT, lambda t: ("hT", t))
        hT_keys = [("hT", t) for t in range(4)]
        if tg == 0:
            dbg_dump("hT", hT, hT_keys, view=lambda d: d)

        if tg > 0:
            T.add("pool", lambda: P_.tensor_copy(out=uT[:, :, 0:30], in_=halo_save[:]),
                  reads=["halo_save"], writes=["uT_halo"])

        T.inherit(YK + SQK, HBK)
        import collections
        pending = collections.deque()
        n_pe = N_PE[tg]

        def conv_dve_ops(c, ai):
            acc, key = cbuf[ai], ("cbuf", ai)
            ops_ = []
            ops_.append(lambda: T.add("dve", lambda: V.tensor_scalar(
                out=acc, in0=uT[:, c, 0:512], scalar1=col(C_W31 + c * 31), scalar2=col(C_CB + c),
                op0=ALU.mult, op1=ALU.add), reads=[("uT", c), "uT_halo", "cst"], writes=[key]))
            for j in range(1, 31):
                ops_.append(lambda j=j: T.add("dve", lambda: V.scalar_tensor_tensor(
                    out=acc, in0=uT[:, c, j:j + 512], scalar=col(C_W31 + c * 31 + j), in1=acc,
                    op0=ALU.mult, op1=ALU.add), reads=[("uT", c), "uT_halo", "cst", key], writes=[key]))
            ops_.append(lambda: T.add("act", lambda: A_.copy(out=ybf[:, c, :], in_=acc),
                                      reads=[key], writes=[("ybf", c)]))
            ops_.append(lambda: T.add("act", lambda: A_.activation(out=ysq_all[:, c, :], in_=acc, func=AF.Square),
                                      reads=[key], writes=[("ysq", c)]))
            return ops_
        n_dve = 8 - n_pe

        def enqueue_pair(p_):
            ca, cb2 = 2 * p_, 2 * p_ + 1
            la = conv_dve_ops(ca, 0) if ca < n_dve else []
            lb = conv_dve_ops(cb2, 1) if cb2 < n_dve else []
            for k_ in range(max(len(la), len(lb))):
                if k_ < len(la):
                    pending.append(la[k_])
                if k_ < len(lb):
                    pending.append(lb[k_])

        def pump(n_):
            for _ in range(n_):
                if pending:
                    pending.popleft()()

        for p in range(4):
            Asl, Ak = ws_next("k16")
            Gsl, Gk = ws_next("k16", held=1)
            for cc in range(2):
                c = 2 * p + cc
                ba = next_bank()
                bg = next_bank()
                mm_group(ps[:, ba, :], [(Asl[:, kt, cc * 128:(cc + 1) * 128], hT[:, kt, :]) for kt in range(16)],
                         reads=[Ak] + hT_keys, writes=[("ps", ba)])
                mm_group(ps[:, bg, :], [(Gsl[:, kt, cc * 128:(cc + 1) * 128], hT[:, kt, :]) for kt in range(16)],
                         reads=[Gk] + hT_keys, writes=[("ps", bg)])
                sb_ = c % 2
                T.add("act", (lambda bg=bg, sb_=sb_: A_.activation(out=sig[sb_], in_=ps[:, bg, :], func=AF.Sigmoid)),
                      reads=[("ps", bg)], writes=[("sig", sb_)])
                T.add("dve", (lambda ba=ba, sb_=sb_, c=c: V.tensor_tensor(out=uT[:, c, 30:542], in0=ps[:, ba, :],
                                                                          in1=sig[sb_], op=ALU.mult)),
                      reads=[("ps", ba), ("sig", sb_)], writes=[("uT", c)])
                pump(8)
            enqueue_pair(p)
        for s in range(8):
            sl, sk = ws_next("k16")
            for cc in range(2):
                h = (2 * s + cc) % 8
                b = next_bank()
                mm_group(ps[:, b, :], [(sl[:, kt, cc * 128:(cc + 1) * 128], hT[:, kt, :]) for kt in range(16)],
                         reads=[sk] + hT_keys, writes=[("ps", b)])
                if s < 4:
                    dst, dk = qT[:, h, :], ("qT", h)
                else:
                    dst, dk = kT[:, h, T0:T0 + 512], ("kT", h, tg)
                if cc == 0:
                    T.add("act", (lambda b=b, dst=dst: A_.copy(out=dst, in_=ps[:, b, :])),
                          reads=[("ps", b)], writes=[dk])
                else:
                    T.add("dve", (lambda b=b, dst=dst: V.tensor_copy(out=dst, in_=ps[:, b, :])),
                          reads=[("ps", b)], writes=[dk])
                pump(4)
        def build_diag(c, db):
            w31c = cst[:, C_W31 + c * 31:C_W31 + (c + 1) * 31]
            T.add("dve", (lambda db=db, w31c=w31c: V.tensor_tensor(
                out=diag[db], in0=ident[:].unsqueeze(1).to_broadcast([128, 31, 128]),
                in1=w31c.unsqueeze(2).to_broadcast([128, 31, 128]), op=ALU.mult)),
                reads=["ident", "cst"], writes=[f"diag{db}"])
        if n_pe > 0:
            T.inherit(["diag0"], [("attnT", t) for t in range(4)])
            build_diag(8 - n_pe, 0)
        for s in range(4):
            sl, sk = ws_next("k16")
            for tt in range(4):
                b = next_bank()
                mm_group(ps[:, b, 0:256], [(hT[:, kt, tt * 128:(tt + 1) * 128], sl[:, kt, :]) for kt in range(16)],
                         reads=[sk, ("hT", tt)], writes=[("ps", b)])
                dst = Vaug[:, tg * 4 + tt, 2 * s:2 * s + 2, 0:128]
                src = ps[:, b, 0:256].rearrange("p (h d) -> p h d", d=128)
                if tt % 2 == 0:
                    T.add("act", (lambda dst=dst, src=src: A_.copy(out=dst, in_=src)),
                          reads=[("ps", b)], writes=[("V", tg * 4 + tt)])
                else:
                    T.add("dve", (lambda dst=dst, src=src: V.tensor_copy(out=dst, in_=src)),
                          reads=[("ps", b)], writes=[("V", tg * 4 + tt)])
                pump(1)
        if tg == 0:
            dbg_dump("uT", uT[:, :, 30:542], [("uT", c_) for c_ in range(8)])
            dbg_dump("qT", qT, [("qT", h) for h in range(8)])
            dbg_dump("kT", kT[:, :, 0:512], [("kT", h, 0) for h in range(8)])
            dbg_dump("V", Vaug[:, 0:4, :, 0:128], [("V", t) for t in range(4)])

        T.inherit(["diag1"], hT_keys)
        for ci_, c in enumerate(range(8 - n_pe, 8)):
            db = ci_ % 2
            if ci_ > 0:
                build_diag(c, db)
            b = next_bank(4)
            mm_group(ps[:, b, :], [(diag[db][:, j, :], uT[:, c, j:j + 512]) for j in range(31)],
                     reads=[f"diag{db}", ("uT", c), "uT_halo"], writes=[("ps", b)])
            T.add("act", (lambda b=b, c=c: A_.activation(out=ybf[:, c, :], in_=ps[:, b, :], func=AF.Identity,
                                                         bias=col(C_CB + c))),
                  reads=[("ps", b), "cst"], writes=[("ybf", c)])
            T.add("act", (lambda b=b, c=c: A_.activation(out=ysq_all[:, c, :], in_=ps[:, b, :], func=AF.Square,
                                                         bias=col(C_CB + c))),
                  reads=[("ps", b), "cst"], writes=[("ysq", c)])
            pump(2)

        T.inherit([("PT", b) for b in range(4)] + ["attn_tok"], hT_keys + ["diag0", "diag1"])
        T.inherit([("osb", i) for i in range(4)] + ["t0b", "Osb"], ["gfin", ("cbuf", 0), ("cbuf", 1), ("cbuf", 2)])
        nj = 4 * tg + 4
        steps = [(h, j) for h in range(8) for j in range(nj)]

        def emit_ST(k):
            h, j = steps[k]
            pb = k % 2
            c0 = max(0, j - 4 * tg) * 128

            def fn():
                i0 = PE.matmul(ps[:, 2 * pb, c0:512], kT[0:64, h, j * 128:(j + 1) * 128], qT[0:64, h, c0:512],
                               start=True, stop=True)
                i1 = PE.matmul(ps[:, 2 * pb + 1, c0:512], kT[64:128, h, j * 128:(j + 1) * 128],
                               qT[64:128, h, c0:512], start=True, stop=True)
                return i0, i1
            T.add("pe", fn, reads=[("kT", h, j // 4), ("qT", h)], writes=[("ps", 2 * pb), ("ps", 2 * pb + 1)])

        emit_ST(0)
        deferred = [None]
        for k, (h, j) in enumerate(steps):
            if k + 1 < len(steps):
                emit_ST(k + 1)
            pb = k % 2
            ptb = k % 4
            r = j - 4 * tg
            c0 = max(0, r) * 128
            T.add("act", (lambda pb=pb, ptb=ptb, c0=c0: A_.activation(out=PT[ptb][:, :, c0:512],
                                                                       in_=ps[:, 2 * pb:2 * pb + 2, c0:512],
                                                                       func=AF.Exp, scale=0.125)),
                  reads=[("ps", 2 * pb), ("ps", 2 * pb + 1)], writes=[("PT", ptb)])
            if r >= 0:
                T.add("pool", (lambda ptb=ptb, c0=c0: P_.memset(PT[ptb][64:128, :, c0:c0 + 64], 0.0)),
                      writes=[("PT", ptb)])

            if deferred[0] is not None:
                deferred[0]()
                deferred[0] = None

            def pv(h=h, j=j, ptb=ptb, r=r):
                first = last = None
                for il in range(max(0, r), 4):
                    for m in range(2):
                        bank = 4 + 2 * m + il // 2
                        o_ap = ps[:, bank, (il % 2) * 129:(il % 2) * 129 + 129]
                        ins = PE.matmul(o_ap, PT[ptb][:, m, il * 128:(il + 1) * 128], Vaug[:, j, h, :],
                                        start=(j == 0 and il % 2 == 0), stop=(j == 4 * tg + il),
                                        skip_group_check=True)
                        if first is None:
                            first = ins
                        last = ins
                return first, last
            T.add("pe", pv, reads=[("PT", ptb), ("V", j)], writes=[("ps", 4), ("ps", 5), ("ps", 6), ("ps", 7)])
            if j < nj - 2:
                pump(1)

            def oproc(h=h):
                okeys = [("ps", 4), ("ps", 5), ("ps", 6), ("ps", 7)]
                if tg < 2:
                    T.add("act", lambda: A_.copy(out=Osb, in_=ps[:, 4:8, 0:258]), reads=okeys, writes=["Osb"])
                else:
                    T.add("dve", lambda: V.tensor_copy(out=Osb, in_=ps[:, 4:8, 0:258]), reads=okeys, writes=["Osb"])
                Ov = Osb.rearrange("p b (s c) -> p b s c", c=129)
                O0 = Ov[:, 0:2, :, 0:128]
                O1 = Ov[:, 2:4, :, 0:128]
                osb4 = osb.rearrange("p (a s) d -> p a s d", s=2)
                t04 = t0b4.rearrange("p (a s) d -> p a s d", s=2)
                rec3 = rec[:].rearrange("p (b s) -> p b s", s=2)
                T.add("dve", lambda: V.reciprocal(out=rec3, in_=Ov[:, :, :, 128]), reads=["Osb"], writes=["rec"])
                T.add("dve", lambda: V.tensor_tensor(out=t04, in0=O0, in1=rec3[:, 0:2, :].unsqueeze(3).to_broadcast([128, 2, 2, 128]),
                                                     op=ALU.mult),
                      reads=["Osb", "rec"], writes=["t0b"])
                T.add("dve", lambda: V.tensor_tensor(out=osb4, in0=O1, in1=rec3[:, 2:4, :].unsqueeze(3).to_broadcast([128, 2, 2, 128]),
                                                     op=ALU.mult),
                      reads=["Osb", "rec"], writes=[("osb", i) for i in range(4)])
                T.add("dve", lambda: V.scalar_tensor_tensor(out=osb, in0=osb, scalar=lamt[:, 5:6], in1=t0b4,
                                                            op0=ALU.mult, op1=ALU.add),
                      reads=[("osb", i) for i in range(4)] + ["t0b", "neglam"], writes=[("osb", i) for i in range(4)])
                if tg < 2:
                    def sqf():
                        first = last = None
                        for il in range(4):
                            ins = A_.activation(out=t0b4[:, il, :], in_=osb[:, il, :], func=AF.Square,
                                                accum_out=st[:, 40 + il:41 + il])
                            if first is None:
                                first = ins
                            last = ins
                        return first, last
                    T.add("act", sqf, reads=[("osb", i) for i in range(4)], writes=["t0b", ("st", 40)])
                else:
                    T.add("dve", lambda: V.tensor_tensor(out=t0b4, in0=osb, in1=osb, op=ALU.mult),
                          reads=[("osb", i) for i in range(4)], writes=["t0b"])
                    T.add("dve", lambda: V.reduce_sum(out=st[:, 40:44], in_=t0b4, axis=mybir.AxisListType.X),
                          reads=["t0b"], writes=[("st", 40)])
                T.add("dve", lambda: V.tensor_scalar(out=st[:, 44:48], in0=st[:, 40:44], scalar1=1.0 / 128,
                                                     scalar2=NORM_EPS, op0=ALU.mult, op1=ALU.add),
                      reads=[("st", 40)], writes=[("st", 44)])
                T.add("pool", lambda: P_.tensor_tensor(out=st[:, 48:52], in0=st[:, 44:48], in1=mhalf[:, 0:4], op=ALU.pow),
                      reads=[("st", 44), "mhalf"], writes=[("st", 48)])
                T.add("dve", (lambda h=h: V.tensor_tensor(out=attn_tok[:, :, h * 128:(h + 1) * 128], in0=osb,
                                                          in1=st[:, 48:52].unsqueeze(2).to_broadcast([128, 4, 128]),
                                                          op=ALU.mult)),
                      reads=[("osb", i) for i in range(4)] + [("st", 48)], writes=["attn_tok"])
                if tg == 0 and h == 0:
                    dbg_dump("rec", rec[:], ["rec"])
                    dbg_dump("osb", osb, [("osb", i) for i in range(4)])
                    dbg_dump("st", st[:], [("st", 48), ("st", 44)])
                    dbg_dump("lamt", lamt[:], ["neglam"])
            if j == nj - 1:
                deferred[0] = oproc
        if deferred[0] is not None:
            deferred[0]()
            deferred[0] = None
        if tg == 0:
            dbg_dump("attn_tok", attn_tok, ["attn_tok"], view=lambda d: d)
        T.inherit([("attnT", t) for t in range(4)], ["diag0"])
        for il in range(4):
            def tr(il=il):
                first = last = None
                for c in range(8):
                    ins = PE.transpose(psT8[:, c, :], attn_tok[:, il, c * 128:(c + 1) * 128], ident[:])
                    if first is None:
                        first = ins
                    last = ins
                return first, last
            T.add("pe", tr, reads=["attn_tok", "ident"], writes=[("ps", 0)])
            T.add("dve", (lambda il=il: V.tensor_scalar(out=attnT[:, :, il * 128:(il + 1) * 128], in0=psT8, scalar1=gs[:, 0:1],
                                                           scalar2=None, op0=ALU.mult)),
                  reads=[("ps", 0), "gs"], writes=[("attnT", il)])

        pump(10 ** 6)
        for c in range(8):
            T.add("pe", (lambda c=c: PE.matmul(ps[:, 4, :], ones[:], ybf[:, c, :], start=(c == 0), stop=(c == 7))),
                  reads=[("ybf", c), "ones"], writes=[("ps", 4)])
        for c in range(8):
            T.add("pe", (lambda c=c: PE.matmul(ps[:, 5, :], ones[:], ysq_all[:, c, :], start=(c == 0), stop=(c == 7))),
                  reads=[("ysq", c), "ones"], writes=[("ps", 5)])
        T.inherit(STK, SQK)
        T.add("pool", lambda: P_.tensor_copy(out=halo_save[:], in_=uT[:, :, 512:542]),
              reads=[("uT", c_) for c_ in range(8)], writes=["halo_save"])
        mean, ex2, rstd_t = stat[0], stat[1], stat[2]
        T.add("dve", lambda: V.tensor_scalar(out=mean, in0=ps[:, 4, :], scalar1=1.0 / 1024, scalar2=None, op0=ALU.mult),
              reads=[("ps", 4)], writes=[("stat", 0)])
        T.add("dve", lambda: V.tensor_scalar(out=ex2, in0=ps[:, 5, :], scalar1=1.0 / 1024, scalar2=LN_EPS,
                                             op0=ALU.mult, op1=ALU.add),
              reads=[("ps", 5)], writes=[("stat", 1)])
        T.add("dve", lambda: V.tensor_tensor(out=rstd_t, in0=mean, in1=mean, op=ALU.mult),
              reads=[("stat", 0)], writes=[("stat", 2)])
        T.add("dve", lambda: V.tensor_tensor(out=ex2, in0=ex2, in1=rstd_t, op=ALU.subtract),
              reads=[("stat", 1), ("stat", 2)], writes=[("stat", 1)])
        T.add("act", lambda: A_.activation(out=rstd_t, in_=ex2, func=AF.Sqrt),
              reads=[("stat", 1)], writes=[("stat", 2)])
        T.add("dve", lambda: V.reciprocal(out=rstd_t, in_=rstd_t),
              reads=[("stat", 2)], writes=[("stat", 2)])
        for c in range(8):
            zb = cbuf[c % 3]
            zk = ("cbuf", c % 3)
            T.add("dve", (lambda c=c, zb=zb: V.tensor_tensor(out=zb, in0=ybf[:, c, :], in1=mean, op=ALU.subtract)),
                  reads=[("ybf", c), ("stat", 0)], writes=[zk])
            T.add("dve", (lambda zb=zb: V.tensor_tensor(out=zb, in0=zb, in1=rstd_t, op=ALU.mult)),
                  reads=[zk, ("stat", 2)], writes=[zk])
            T.add("act", (lambda c=c, zb=zb: A_.activation(out=uT[:, c, 30:542], in_=zb, func=AF.Silu,
                                                           scale=col(C_LNG + c), bias=col(C_LNB + c))),
                  reads=[zk, "cst", "halo_save"], writes=[("uT", c)])
        if tg == 0:
            dbg_dump("convT", uT[:, :, 30:542], [("uT", c_) for c_ in range(8)])
            dbg_dump("attnT", attnT, [("attnT", t) for t in range(4)])

        UTKS = [("uT", c_) for c_ in range(8)]
        wo = {}

        def wo_attn(s_, tt, sl, sk):
            b = next_bank(6)
            wo[(s_, tt)] = b
            mm_group(ps[:, b, 0:256], [(attnT[:, c, tt * 128:(tt + 1) * 128], sl[:, 8 + c, :]) for c in range(8)],
                     reads=[sk, ("attnT", tt)], writes=[("ps", b)], stop_last=False)

        def wo_conv(s_, tt, sl, sk):
            b = wo[(s_, tt)]

            def fn():
                first = last = None
                for c in range(8):
                    ins = PE.matmul(ps[:, b, 0:256], uT[:, c, 30 + tt * 128:30 + (tt + 1) * 128], sl[:, c, :],
                                    start=False, stop=(c == 7))
                    if first is None:
                        first = ins
                    last = ins
                return first, last
            T.add("pe", fn, reads=[sk] + UTKS, writes=[("ps", b)])
            T.add("dve", (lambda: V.tensor_tensor(out=xres[:, tt, s_ * 256:(s_ + 1) * 256], in0=ps[:, b, 0:256],
                                                  in1=xres[:, tt, s_ * 256:(s_ + 1) * 256], op=ALU.add)),
                  reads=[("ps", b), ("xres", tt)], writes=[("xres", tt)])
        sl0, sk0 = ws_next("k16")
        sl1, sk1 = ws_next("k16", held=1)
        for tt in range(4):
            wo_attn(0, tt, sl0, sk0)
        for tt in range(2):
            wo_attn(1, tt, sl1, sk1)
        for tt in range(4):
            wo_conv(0, tt, sl0, sk0)
        for tt in range(2):
            wo_conv(1, tt, sl1, sk1)
        for tt in range(2, 4):
            wo_attn(1, tt, sl1, sk1)
            wo_conv(1, tt, sl1, sk1)
        for s in range(2, 8):
            sl, sk = ws_next("k16")
            for tt in range(4):
                wo_attn(s, tt, sl, sk)
                wo_conv(s, tt, sl, sk)
        if tg == 0:
            dbg_dump("x1", xres[:, 0, :], [("xres", 0)])

        T.inherit([("h2T", t) for t in range(4)], hT_keys + ["diag0", "diag1"] + [("PT", b) for b in range(4)] + ["attn_tok"])
        T.inherit(HBK, YK + SQK + STK)
        for tt in range(4):
            norm_transpose(xres[:, tt, :], ("xres", tt), tt, C_GFFN, hT, lambda t: ("h2T", t))
        h2_keys = [("h2T", t) for t in range(4)]

        T.inherit(["act"], [("qT", h) for h in range(8)] + [("uT", c_) for c_ in range(8)] + ["uT_halo"] + [("attnT", t) for t in range(4)])
        T.inherit(["gfin", ("cbuf", 0), ("cbuf", 1), ("cbuf", 2)], [("osb", i) for i in range(4)] + ["t0b", "Osb"])
        T.add("sp", lambda: SP.dma_start(out=gfin, in_=gfin_d[:, :]), writes=["gfin"], dma=True)
        for hh in range(2):
            for p in range(11):
                Gsl, Gk = ws_next("k16")
                Usl, Uk = ws_next("k16", held=1)
                for cc in range(2):
                    fl = 2 * p + cc
                    fc = hh * 22 + fl
                    pr = next_bank(4)
                    bg, bu = 2 * pr, 2 * pr + 1
                    bank_rr[0] += 0
                    mm_group(ps[:, bg, :], [(Gsl[:, kt, cc * 128:(cc + 1) * 128], hT[:, kt, :]) for kt in range(16)],
                             reads=[Gk] + h2_keys, writes=[("ps", bg)])
                    mm_group(ps[:, bu, :], [(Usl[:, kt, cc * 128:(cc + 1) * 128], hT[:, kt, :]) for kt in range(16)],
                             reads=[Uk] + h2_keys, writes=[("ps", bu)])
                    ci = fc % 3
                    cb_, ck = cbuf[ci], ("cbuf", ci)
                    w0, w1, w2 = col(C_FW + fc * 3), col(C_FW + fc * 3 + 1), col(C_FW + fc * 3 + 2)
                    T.add("act", (lambda bg=bg, cb_=cb_, w2=w2, fc=fc: A_.activation(out=cb_, in_=ps[:, bg, :],
                                                                                     func=AF.Identity, scale=w2,
                                                                                     bias=col(C_FB + fc))),
                          reads=[("ps", bg), "cst"], writes=[ck])
                    T.add("dve", (lambda bg=bg, cb_=cb_, w1=w1: V.scalar_tensor_tensor(out=cb_[:, 1:512],
                                                                                       in0=ps[:, bg, 0:511], scalar=w1,
                                                                                       in1=cb_[:, 1:512],
                                                                                       op0=ALU.mult, op1=ALU.add)),
                          reads=[("ps", bg), ck, "cst"], writes=[ck])
                    T.add("dve", (lambda bg=bg, cb_=cb_, w0=w0: V.scalar_tensor_tensor(out=cb_[:, 2:512],
                                                                                       in0=ps[:, bg, 0:510], scalar=w0,
                                                                                       in1=cb_[:, 2:512],
                                                                                       op0=ALU.mult, op1=ALU.add)),
                          reads=[("ps", bg), ck, "cst"], writes=[ck])
                    T.add("dve", (lambda cb_=cb_, w0=w0, fc=fc: V.scalar_tensor_tensor(out=cb_[:, 0:2],
                                                                                       in0=fhalo[:, fc, 0:2], scalar=w0,
                                                                                       in1=cb_[:, 0:2],
                                                                                       op0=ALU.mult, op1=ALU.add)),
                          reads=["fhalo", ck, "cst"], writes=[ck])
                    T.add("dve", (lambda cb_=cb_, w1=w1, fc=fc: V.scalar_tensor_tensor(out=cb_[:, 0:1],
                                                                                       in0=fhalo[:, fc, 1:2], scalar=w1,
                                                                                       in1=cb_[:, 0:1],
                                                                                       op0=ALU.mult, op1=ALU.add)),
                          reads=["fhalo", ck, "cst"], writes=[ck])
                    T.add("dve", (lambda bg=bg, fc=fc: V.tensor_copy(out=fhalo[:, fc, :], in_=ps[:, bg, 510:512])),
                          reads=[("ps", bg)], writes=["fhalo"])
                    T.add("act", (lambda cb_=cb_: A_.activation(out=cb_, in_=cb_, func=AF.Gelu)),
                          reads=[ck], writes=[ck])
                    T.add("dve", (lambda cb_=cb_, bu=bu, fl=fl: V.tensor_tensor(out=act[:, fl, :], in0=ps[:, bu, :],
                                                                                in1=cb_, op=ALU.mult)),
                          reads=[("ps", bu), ck], writes=["act"])
            if tg == 0 and hh == 0:
                dbg_dump("act", act[:, 0:4, :], ["act"])
            for dq in range(4):
                base = (dq % 2) * 4
                bks = [("ps", base + tt) for tt in range(4)]
                for (f0, nf) in ((0, 8), (8, 8), (16, 6)):
                    sl, sk = ws_next("f")

                    def dn(sl=sl, f0=f0, nf=nf, base=base):
                        first = last = None
                        for fi in range(nf):
                            fl = f0 + fi
                            for tt in range(4):
                                ins = PE.matmul(ps[:, base + tt, :], act[:, fl, tt * 128:(tt + 1) * 128], sl[:, fi, :],
                                                start=(fl == 0), stop=(fl == 21))
                                if first is None:
                                    first = ins
                                last = ins
                        return first, last
                    T.add("pe", dn, reads=[sk, "act"], writes=bks)
                for tt in range(4):
                    T.add("dve", (lambda tt=tt, dq=dq, base=base: V.tensor_tensor(
                        out=xres[:, tt, dq * 512:(dq + 1) * 512], in0=ps[:, base + tt, :],
                        in1=xres[:, tt, dq * 512:(dq + 1) * 512], op=ALU.add)),
                        reads=[("ps", base + tt), ("xres", tt)], writes=[("xres", tt)])

        for tt in range(4):
            b = tt % 2
            c0 = 8 * tt
            rms_stats(xres[:, tt, :], ("xres", tt), hb[b], ("hb", b), c0, 1.0 / D, NORM_EPS)
            T.add("dve", (lambda tt=tt, c0=c0: V.scalar_tensor_tensor(out=xres[:, tt, :], in0=xres[:, tt, :],
                                                                      scalar=st[:, c0 + 2:c0 + 3], in1=gfin,
                                                                      op0=ALU.mult, op1=ALU.mult)),
                  reads=[("xres", tt), ("st", c0 + 2), "gfin"], writes=[("xres", tt)])
            r0 = T0 + tt * 128
            T.add("sp", (lambda tt=tt, r0=r0: SP.dma_start(out=out_d[r0:r0 + 128, :], in_=xres[:, tt, :])),
                  reads=[("xres", tt)], dma=True)
            if tg + 1 < NG:
                T.add("sp", (lambda tt=tt, r1=r0 + 512: SP.dma_start(out=xres[:, tt, :], in_=x_d[r1:r1 + 128, :])),
                      writes=[("xres", tt)], dma=True)

    assert ws_state["used"] == len(slabs), (ws_state, len(slabs))
    nw = T.emit()
    for q, sems in T.dsem.items():
        for sl, sem in enumerate(sems):
            v = T.dlast_val[q][sl]
            if v > 0:
                nc.sync.wait_ge(sem, v)
    nc.all_engine_barrier()
    allsems = list(T.esem.values()) + [sm for sems in T.dsem.values() for sm in sems]
    nc.clear_and_free_semaphores(allsems)
    nc.all_engine_barrier()
    return nc, T, nw


_CACHE = {}


def _prep_consts(inp):
    c = np.zeros((128, C_TOT), np.float32)
    c[:, C_GMIX:C_GMIX + 16] = inp["norm_mix_g"][0].reshape(16, 128).T
    c[:, C_GFFN:C_GFFN + 16] = inp["norm_ffn_g"][0].reshape(16, 128).T
    w31 = inp["conv_dw_w"][0].reshape(31, 8, 128).transpose(2, 1, 0)
    c[:, C_W31:C_W31 + 248] = w31.reshape(128, 248)
    c[:, C_CB:C_CB + 8] = inp["conv_dw_b"][0].reshape(8, 128).T
    c[:, C_LNG:C_LNG + 8] = inp["conv_ln_g"][0].reshape(8, 128).T
    c[:, C_LNB:C_LNB + 8] = inp["conv_ln_b"][0].reshape(8, 128).T
    fw = inp["ffn_dw_w"][0].reshape(3, NFC, 128).transpose(2, 1, 0)
    c[:, C_FW:C_FW + 132] = fw.reshape(128, 132)
    c[:, C_FB:C_FB + NFC] = inp["ffn_dw_b"][0].reshape(NFC, 128).T
    c[:, C_LQ1:C_LQ1 + 64] = np.broadcast_to(inp["lambda_q1"][0], (128, 64))
    c[:, C_LK1:C_LK1 + 64] = np.broadcast_to(inp["lambda_k1"][0], (128, 64))
    c[:, C_LQ2:C_LQ2 + 64] = np.broadcast_to(inp["lambda_q2"][0], (128, 64))
    c[:, C_LK2:C_LK2 + 64] = np.broadcast_to(inp["lambda_k2"][0], (128, 64))
    c[:, C_SUB] = inp["subln_g"][0]
    return c


def kernel(**inputs):
    inp = {k: np.asarray(v, dtype=np.float32) for k, v in inputs.items()}
    debug = _CACHE.get("debug")
    nc = build_full(debug)
    consts = _prep_consts(inp)
    gfin = np.ascontiguousarray(np.broadcast_to(inp["norm_final_g"].reshape(1, D), (128, D)))
    shared = {
        "w_in": np.ascontiguousarray(inp["w_in"][0]),
        "w_out": np.ascontiguousarray(inp["w_out"][0]),
        "w_up": np.ascontiguousarray(inp["w_up"][0]),
        "w_down": np.ascontiguousarray(inp["w_down"][0]),
        "consts": consts,
        "gfin": gfin,
    }
    ncores = _CACHE.get("ncores", NCORES)
    in_maps = []
    for c in range(ncores):
        m = dict(shared)
        m["x"] = np.ascontiguousarray(inp["x"][c])
        in_maps.append(m)
    res = run_bass_kernel_spmd(nc, in_maps, core_ids=list(range(ncores)))
    _CACHE["last_results"] = res.results
    out = np.stack([np.asarray(r["out"], dtype=np.float32).reshape(SEQ, D) for r in res.results], axis=0)
    if ncores < NCORES:
        full = np.zeros((NCORES, SEQ, D), np.float32)
        full[:ncores] = out
        out = full
    return out


def build_full(debug=None):
    nc, T, nw = build(debug)
    return nc
```

```python
import numpy as np
import concourse.bass as bass
import concourse.mybir as mybir
from concourse.bass_utils import run_bass_kernel_spmd

F32 = mybir.dt.float32
BF16 = mybir.dt.bfloat16
AF = mybir.ActivationFunctionType
ALU = mybir.AluOpType

D = 2048
SEQ = 2048
NCORES = 8
DIN = 5120
DFF = 5632
NG = 4
NFC = 44
NORM_EPS = 1e-6
LN_EPS = 1e-5
LAMBDA_INIT = 0.2

C_GMIX, C_GFFN, C_W31, C_CB, C_LNG, C_LNB = 0, 16, 32, 280, 288, 296
C_FW, C_FB, C_LQ1, C_LK1, C_LQ2, C_LK2, C_SUB, C_TOT = 304, 436, 480, 544, 608, 672, 736, 737


class Tracker:
    def __init__(self, nc):
        self.nc = nc
        self.ops = []
        self.lastw = {}
        self.readers = {}

    def add(self, eng, fn, reads=(), writes=(), dma=False):
        idx = len(self.ops)
        deps = set()
        for k in reads:
            w = self.lastw.get(k)
            if w is not None:
                deps.add(w)
        for k in writes:
            w = self.lastw.get(k)
            if w is not None:
                deps.add(w)
            for r in self.readers.get(k, {}).values():
                deps.add(r)
        for k in reads:
            rk = ("dma", idx) if dma else eng
            self.readers.setdefault(k, {})[rk] = idx
        for k in writes:
            self.lastw[k] = idx
            self.readers[k] = {}
        deps.discard(idx)
        self.ops.append(dict(eng=eng, fn=fn, deps=deps, dma=dma))
        return idx

    def inherit(self, new_keys, old_keys):
        for nk in new_keys:
            rd = self.readers.setdefault(nk, {})
            for ok in old_keys:
                w = self.lastw.get(ok)
                if w is not None:
                    rd[("al", ok, "w")] = w
                for rk, r in self.readers.get(ok, {}).items():
                    rd[("al", ok, rk)] = r

    def emit(self):
        nc = self.nc
        engs = {"pe": nc.tensor, "act": nc.scalar, "dve": nc.vector,
                "pool": nc.gpsimd, "sp": nc.sync}
        ops = self.ops
        n = len(ops)
        need = [False] * n
        for op in ops:
            for d in op["deps"]:
                need[d] = True
        esem = {e: nc.alloc_semaphore("es_" + e) for e in ("pe", "act", "dve", "pool")}
        NS = {"sp": 8, "pool": 8, "act": 4}
        dsem = {q: [nc.alloc_semaphore(f"ds_{q}_{i}") for i in range(NS[q])] for q in NS}
        ecount = {e: 0 for e in esem}
        dcount = {q: 0 for q in NS}
        dlast = {q: [None] * NS[q] for q in NS}
        eclock = {e: {} for e in engs}
        token = [None] * n
        opclock = [None] * n
        nwaits = 0
        for i, op in enumerate(ops):
            e = op["eng"]
            eng = engs[e]
            clk = eclock[e]
            cands = []
            for d in op["deps"]:
                if (not op["dma"]) and e == "pe" and ops[d]["eng"] == "pe" and not ops[d]["dma"]:
                    continue
                cands.append(d)
            my_tok = None
            if op["dma"]:
                j = dcount[e]
                slot = j % NS[e]
                prev = dlast[e][slot]
                if prev is not None:
                    cands.append(prev)
                my_tok = (("d", e, slot), 16 * (j // NS[e] + 1))
                dcount[e] += 1
                dlast[e][slot] = i
            cands.sort(key=lambda d: -token[d][1])
            waits = {}
            for d in cands:
                sk, val = token[d]
                if clk.get(sk, 0) >= val:
                    continue
                waits[sk] = max(waits.get(sk, 0), val)
                for k2, v2 in opclock[d].items():
                    if clk.get(k2, 0) < v2:
                        clk[k2] = v2
            wl = [(sk, v) for sk, v in waits.items()]

            def semof(sk):
                return esem[sk[1]] if sk[0] == "e" else dsem[sk[1]][sk[2]]

            for sk, v in wl[1:]:
                eng.wait_ge(semof(sk), v)
                nwaits += 1
            r = op["fn"]()
            if isinstance(r, tuple):
                first, last = r
            else:
                first = last = r
            if wl:
                first._wait_ge(semof(wl[0][0]), wl[0][1])
            if op["dma"]:
                last.then_inc(semof(my_tok[0]), 16)
                token[i] = my_tok
                oc = dict(clk)
                oc[my_tok[0]] = my_tok[1]
                opclock[i] = oc
            else:
                if need[i]:
                    ecount[e] += 1
                    last.then_inc(esem[e], 1)
                token[i] = (("e", e), ecount[e])
                oc = dict(clk)
                oc[("e", e)] = ecount[e]
                opclock[i] = oc
        self.final_tokens = (token, opclock)
        self.dsem = dsem
        self.esem = esem
        self.dlast_val = {q: [ (16 * ((dcount[q] - 1 - sl) // NS[q] + 1) if dcount[q] > sl else 0) for sl in range(NS[q])] for q in NS}
        return nwaits


def build(debug=False):
    nc = bass.Bass("TRN2", target_bir_lowering=False)
    T = Tracker(nc)

    x_d = nc.dram_tensor("x", [SEQ, D], F32, kind="ExternalInput").ap()
    win_d = nc.dram_tensor("w_in", [D, DIN], F32, kind="ExternalInput").ap()
    wout_d = nc.dram_tensor("w_out", [D, D], F32, kind="ExternalInput").ap()
    wup_d = nc.dram_tensor("w_up", [D, 2 * DFF], F32, kind="ExternalInput").ap()
    wdn_d = nc.dram_tensor("w_down", [DFF, D], F32, kind="ExternalInput").ap()
    cst_d = nc.dram_tensor("consts", [128, C_TOT], F32, kind="ExternalInput").ap()
    gfin_d = nc.dram_tensor("gfin", [128, D], F32, kind="ExternalInput").ap()
    out_d = nc.dram_tensor("out", [SEQ, D], F32, kind="ExternalOutput").ap()
    dbg = {}
    if debug:
        for name, shape in debug.items():
            dbg[name] = nc.dram_tensor("dbg_" + name, list(shape), F32, kind="ExternalOutput").ap()

    win_v = win_d.rearrange("(kt p) f -> p kt f", p=128)
    wout_v = wout_d.rearrange("(kt p) f -> p kt f", p=128)
    wup_v = wup_d.rearrange("(kt p) f -> p kt f", p=128)
    wdn_v = wdn_d.rearrange("(fc p) d -> p fc d", p=128)

    cst = nc.alloc_sbuf_tensor("cst", [128, C_TOT], F32)
    kT = nc.alloc_sbuf_tensor("kT", [128, 8, SEQ], BF16)
    Vaug = nc.alloc_sbuf_tensor("Vaug", [128, 16, 8, 129], BF16)
    xres = nc.alloc_sbuf_tensor("xres", [128, 4, D], F32)
    R = nc.alloc_sbuf_tensor("R", [128, 8192 + 4096 + 4336 + 4096], BF16)
    hT = R[:, 0:8192].rearrange("p (k t) -> p k t", t=512)
    qT = R[:, 8192:12288].rearrange("p (h t) -> p h t", t=512)
    uT = R[:, 12288:16624].rearrange("p (c t) -> p c t", t=542)
    attnT = R[:, 16624:20720].rearrange("p (c t) -> p c t", t=512)
    act = R[:, 8192:8192 + 22 * 512].rearrange("p (f t) -> p f t", t=512)
    PT = [R[:, b * 1024:(b + 1) * 1024].rearrange("p (m t) -> p m t", t=512) for b in range(4)]
    attn_tok = R[:, 4096:8192].rearrange("p (i f) -> p i f", f=1024)
    diag = [R[:, 16624:16624 + 3968].rearrange("p (j m) -> p j m", m=128),
            R[:, 0:3968].rearrange("p (j m) -> p j m", m=128)]
    wring = nc.alloc_sbuf_tensor("wring", [128, 4, 4096], BF16)
    SA = nc.alloc_sbuf_tensor("SA", [128, 3600], F32)
    gfin = SA[:, 0:2048]
    cbuf = [SA[:, 2064 + i * 512:2064 + (i + 1) * 512] for i in range(3)]
    osb = SA[:, 0:512].rearrange("p (i d) -> p i d", d=128)
    t0b4 = SA[:, 512:1024].rearrange("p (i d) -> p i d", d=128)
    Osb = SA[:, 1024:1024 + 1032].rearrange("p (b c) -> p b c", c=258)
    SB = nc.alloc_sbuf_tensor("SB", [128, 8192], BF16)
    hb = [SB[:, 0:2048], SB[:, 2048:4096]]
    hbN = [R[:, 8192:10240], R[:, 10240:12288]]
    yout = [SB[:, 0:4096].bitcast(F32), SB[:, 4096:8192].bitcast(F32)]
    yjunk = [SB[:, 0:2048], SB[:, 4096:6144]]
    sig = [SB[:, 4096:5120].bitcast(F32), SB[:, 5120:6144].bitcast(F32)]
    ybf = SB[:, 0:4096].rearrange("p (c t) -> p c t", t=512)
    ysq_all = SB[:, 4096:8192].rearrange("p (c t) -> p c t", t=512)
    stat = [SB[:, 4096 + i * 1024:4096 + (i + 1) * 1024].bitcast(F32) for i in range(3)]
    ident = nc.alloc_sbuf_tensor("ident", [128, 128], BF16)
    identf = nc.alloc_sbuf_tensor("identf", [128, 128], F32)
    ones = nc.alloc_sbuf_tensor("ones", [128, 128], BF16)
    mhalf = nc.alloc_sbuf_tensor("mhalf", [128, 8], F32)
    st = nc.alloc_sbuf_tensor("st", [128, 64], F32)
    lamt = nc.alloc_sbuf_tensor("lamt", [128, 8], F32)
    lj = nc.alloc_sbuf_tensor("lj", [128, 64], F32)
    gs = nc.alloc_sbuf_tensor("gs", [128, 1], F32)
    rec = nc.alloc_sbuf_tensor("rec", [128, 8], F32)
    halo_save = nc.alloc_sbuf_tensor("halo_save", [128, 8, 30], BF16)
    fhalo = nc.alloc_sbuf_tensor("fhalo", [128, NFC, 2], F32)

    ps = nc.alloc_psum_tensor("ps", [128, 8, 512], F32)
    psT16 = ps[:, 6:8, :].bitcast(BF16).rearrange("p b (k t) -> p (b k) t", t=128)
    psT8 = ps[:, 0:1, :].bitcast(BF16).rearrange("p b (k t) -> p (b k) t", t=128)

    V, A_, P_, PE, SP = nc.vector, nc.scalar, nc.gpsimd, nc.tensor, nc.sync

    def col(c):
        return cst[:, c:c + 1]

    slabs = []
    for tg in range(NG):
        for p in range(4):
            slabs.append((win_v[:, :, p * 256:(p + 1) * 256], "k16"))
            slabs.append((win_v[:, :, 1024 + p * 256:1024 + (p + 1) * 256], "k16"))
        for s in range(12):
            slabs.append((win_v[:, :, 2048 + s * 256:2048 + (s + 1) * 256], "k16"))
        for s in range(8):
            slabs.append((wout_v[:, :, s * 256:(s + 1) * 256], "k16"))
        for hh in range(2):
            for p in range(11):
                c0 = (hh * 22 + 2 * p) * 128
                slabs.append((wup_v[:, :, c0:c0 + 256], "k16"))
                slabs.append((wup_v[:, :, DFF + c0:DFF + c0 + 256], "k16"))
            for dq in range(4):
                for (f0, nf) in ((0, 8), (8, 8), (16, 6)):
                    slabs.append((wdn_v[:, hh * 22 + f0:hh * 22 + f0 + nf, dq * 512:(dq + 1) * 512], ("f", nf)))
    ws_state = {"issued": 0, "used": 0}

    def ws_issue(upto):
        while ws_state["issued"] < min(upto, len(slabs)):
            n_ = ws_state["issued"]
            src, kind = slabs[n_]
            slot = n_ % 4
            if kind == "k16":
                dst = wring[:, slot, :].rearrange("p (k c) -> p k c", c=256)
            else:
                dst = wring[:, slot, 0:kind[1] * 512].rearrange("p (f c) -> p f c", c=512)
            T.add("pool", (lambda dst=dst, src=src: P_.dma_start(out=dst, in_=src)),
                  writes=[("ws", slot)], dma=True)
            ws_state["issued"] += 1

    def ws_next(kind_expect, held=0):
        n_ = ws_state["used"]
        ws_state["used"] += 1
        ws_issue(n_ - held + 4)
        src, kind = slabs[n_]
        assert (kind == "k16") == (kind_expect == "k16"), (n_, kind, kind_expect)
        slot = n_ % 4
        if kind == "k16":
            view = wring[:, slot, :].rearrange("p (k c) -> p k c", c=256)
        else:
            view = wring[:, slot, 0:kind[1] * 512].rearrange("p (f c) -> p f c", c=512)
        return view, ("ws", slot)

    def mm_group(out_ap, pairs, reads, writes, start_first=True, stop_last=True):
        def fn():
            first = last = None
            nmm = len(pairs)
            for i_, (l_, r_) in enumerate(pairs):
                ins = PE.matmul(out_ap, l_, r_, start=(i_ == 0 and start_first), stop=(i_ == nmm - 1 and stop_last))
                if first is None:
                    first = ins
                last = ins
            return first, last
        T.add("pe", fn, reads=reads, writes=writes)

    def dbg_dump(name, src_ap, reads, view=None):
        if name in dbg:
            dst = dbg[name] if view is None else view(dbg[name])
            T.add("pool", (lambda: P_.dma_start(out=dst, in_=src_ap)), reads=reads, dma=True)

    def rms_stats(src, src_key, junk, junk_key, c0, inv_n, eps):
        T.add("act", lambda: A_.activation(out=junk, in_=src, func=AF.Square, accum_out=st[:, c0:c0 + 1]),
              reads=[src_key], writes=[junk_key, ("st", c0)])
        T.add("dve", lambda: V.tensor_scalar(out=st[:, c0 + 1:c0 + 2], in0=st[:, c0:c0 + 1], scalar1=inv_n,
                                             scalar2=eps, op0=ALU.mult, op1=ALU.add),
              reads=[("st", c0)], writes=[("st", c0 + 1)])
        T.add("pool", lambda: P_.tensor_tensor(out=st[:, c0 + 2:c0 + 3], in0=st[:, c0 + 1:c0 + 2],
                                               in1=mhalf[:, 0:1], op=ALU.pow),
              reads=[("st", c0 + 1), "mhalf"], writes=[("st", c0 + 2)])

    def norm_transpose(src, src_key, tt, gcol, dstT, dst_key, hbufs=None, hname="hb"):
        b = tt % 2
        c0 = 8 * tt
        hbuf = (hbufs or hb)[b]
        hkey = (hname, b)
        rms_stats(src, src_key, hbuf, hkey, c0, 1.0 / D, NORM_EPS)
        T.add("dve", lambda: V.tensor_scalar(out=hbuf, in0=src, scalar1=st[:, c0 + 2:c0 + 3], scalar2=None,
                                             op0=ALU.mult),
              reads=[src_key, ("st", c0 + 2)], writes=[hkey])

        def tr():
            first = last = None
            for k in range(16):
                ins = PE.transpose(psT16[:, k, :], hbuf[:, k * 128:(k + 1) * 128], ident[:])
                if first is None:
                    first = ins
                last = ins
            return first, last
        T.add("pe", tr, reads=[hkey, "ident"], writes=[("ps", 6), ("ps", 7)])
        gT = cst[:, gcol:gcol + 16]
        T.add("dve", lambda: V.tensor_tensor(out=dstT[:, :, tt * 128:(tt + 1) * 128], in0=psT16,
                                             in1=gT.unsqueeze(2).to_broadcast([128, 16, 128]), op=ALU.mult),
              reads=[("ps", 6), ("ps", 7), "cst"], writes=[dst_key(tt)])

    T.add("sp", lambda: SP.dma_start(out=cst[:], in_=cst_d[:, :]), writes=["cst"], dma=True)
    T.add("pool", lambda: P_.memset(identf[:], 0.0), writes=["identf"])
    T.add("pool", lambda: P_.affine_select(out=identf[:], in_=identf[:], pattern=[[-1, 128]],
                                           compare_op=ALU.not_equal, fill=1.0, base=0, channel_multiplier=1),
          reads=["identf"], writes=["identf"])
    T.add("pool", lambda: P_.tensor_copy(out=ident[:], in_=identf[:]), reads=["identf"], writes=["ident"])
    T.add("pool", lambda: P_.memset(ones[:], 1.0), writes=["ones"])
    T.add("pool", lambda: P_.memset(mhalf[:], -0.5), writes=["mhalf"])
    T.add("pool", lambda: P_.memset(Vaug[:, :, :, 128:129], 1.0), writes=[("V", t) for t in range(16)])
    T.add("pool", lambda: P_.memset(uT[:, :, 0:30], 0.0), writes=["uT_halo"])
    T.add("pool", lambda: P_.memset(fhalo[:], 0.0), writes=["fhalo"])
    T.add("dve", lambda: V.scalar_tensor_tensor(out=lj[:], in0=cst[:, C_LQ1:C_LQ1 + 64], scalar=1.0,
                                                in1=cst[:, C_LK1:C_LK1 + 64], op0=ALU.mult, op1=ALU.mult,
                                                accum_out=lamt[:, 0:1]),
          reads=["cst"], writes=["lj", ("lam", 0)])
    T.add("dve", lambda: V.scalar_tensor_tensor(out=lj[:], in0=cst[:, C_LQ2:C_LQ2 + 64], scalar=1.0,
                                                in1=cst[:, C_LK2:C_LK2 + 64], op0=ALU.mult, op1=ALU.mult,
                                                accum_out=lamt[:, 1:2]),
          reads=["cst"], writes=["lj", ("lam", 1)])
    T.add("act", lambda: A_.activation(out=lamt[:, 2:4], in_=lamt[:, 0:2], func=AF.Exp),
          reads=[("lam", 0), ("lam", 1)], writes=[("lam", 2)])
    T.add("dve", lambda: V.tensor_tensor(out=lamt[:, 4:5], in0=lamt[:, 2:3], in1=lamt[:, 3:4], op=ALU.subtract),
          reads=[("lam", 2)], writes=[("lam", 4)])
    T.add("dve", lambda: V.tensor_scalar(out=lamt[:, 5:6], in0=lamt[:, 4:5], scalar1=LAMBDA_INIT, scalar2=-1.0,
                                         op0=ALU.add, op1=ALU.mult),
          reads=[("lam", 4)], writes=["neglam"])
    T.add("dve", lambda: V.tensor_scalar(out=gs[:], in0=cst[:, C_SUB:C_SUB + 1], scalar1=1.0 - LAMBDA_INIT,
                                         scalar2=None, op0=ALU.mult),
          reads=["cst"], writes=["gs"])

    bank_rr = [0]
    YK = [("ybf", c) for c in range(8)]
    SQK = [("ysq", c) for c in range(8)]
    STK = [("stat", i) for i in range(3)]
    HBK = [("hb", 0), ("hb", 1), ("sig", 0), ("sig", 1)]
    YOK = [("yout", 0), ("yout", 1)]
    HNK = [("hbN", 0), ("hbN", 1)]
    N_PE = [4, 3, 2, 1]

    def next_bank(nb=6):
        b = bank_rr[0] % nb
        bank_rr[0] += 1
        return b

    for tg in range(NG):
        T0 = tg * 512
        T.inherit(HBK, YK + SQK + STK + YOK)
        T.inherit(HNK, ["act"])
        T.inherit([("hT", t) for t in range(4)], [("h2T", t) for t in range(4)])
        T.inherit([("qT", h) for h in range(8)] + [("uT", c_) for c_ in range(8)] + ["uT_halo"] + [("attnT", t) for t in range(4)], ["act"])

        if tg == 0:
            for tt in range(4):
                r0 = T0 + tt * 128
                T.add("sp", (lambda tt=tt, r0=r0: SP.dma_start(out=xres[:, tt, :], in_=x_d[r0:r0 + 128, :])),
                      writes=[("xres", tt)], dma=True)
        for tt in range(4):
            norm_transpose(xres[:, tt, :], ("xres", tt), tt, C_GMIX, hT, lambda t: ("hT", t), hbufs=hbN, hname="hbN")
        hT_keys = [("hT", t) for t in range(4)]
        if tg == 0:
            dbg_dump("hT", hT, hT_keys, view=lambda d: d)

        if tg > 0:
            T.add("pool", lambda: P_.tensor_copy(out=uT[:, :, 0:30], in_=halo_save[:]),
                  reads=["halo_save"], writes=["uT_halo"])

        T.inherit(YK + SQK, HBK + YOK)
        T.inherit([("qT", h_) for h_ in range(8)], HNK)
        import collections
        pending = collections.deque()
        n_pe = N_PE[tg]

        def conv_dve_ops(c, ai):
            acc, key = cbuf[ai], ("cbuf", ai)
            ops_ = []
            ops_.append(lambda: T.add("dve", lambda: V.tensor_scalar(
                out=acc, in0=uT[:, c, 0:512], scalar1=col(C_W31 + c * 31), scalar2=col(C_CB + c),
                op0=ALU.mult, op1=ALU.add), reads=[("uT", c), "uT_halo", "cst"], writes=[key]))
            for j in range(1, 31):
                ops_.append(lambda j=j: T.add("dve", lambda: V.scalar_tensor_tensor(
                    out=acc, in0=uT[:, c, j:j + 512], scalar=col(C_W31 + c * 31 + j), in1=acc,
                    op0=ALU.mult, op1=ALU.add), reads=[("uT", c), "uT_halo", "cst", key], writes=[key]))
            ops_.append(lambda: T.add("act", lambda: A_.copy(out=ybf[:, c, :], in_=acc),
                                      reads=[key], writes=[("ybf", c)]))
            ops_.append(lambda: T.add("act", lambda: A_.activation(out=ysq_all[:, c, :], in_=acc, func=AF.Square),
                                      reads=[key], writes=[("ysq", c)]))
            return ops_
        n_dve = 8 - n_pe

        def enqueue_pair(p_):
            ca, cb2 = 2 * p_, 2 * p_ + 1
            la = conv_dve_ops(ca, 0) if ca < n_dve else []
            lb = conv_dve_ops(cb2, 1) if cb2 < n_dve else []
            for k_ in range(max(len(la), len(lb))):
                if k_ < len(la):
                    pending.append(la[k_])
                if k_ < len(lb):
                    pending.append(lb[k_])

        def pump(n_):
            for _ in range(n_):
                if pending:
                    pending.popleft()()

        for p in range(4):
            Asl, Ak = ws_next("k16")
            Gsl, Gk = ws_next("k16", held=1)
            for cc in range(2):
                c = 2 * p + cc
                ba = next_bank()
                bg = next_bank()
                mm_group(ps[:, ba, :], [(Asl[:, kt, cc * 128:(cc + 1) * 128], hT[:, kt, :]) for kt in range(16)],
                         reads=[Ak] + hT_keys, writes=[("ps", ba)])
                mm_group(ps[:, bg, :], [(Gsl[:, kt, cc * 128:(cc + 1) * 128], hT[:, kt, :]) for kt in range(16)],
                         reads=[Gk] + hT_keys, writes=[("ps", bg)])
                sb_ = c % 2
                T.add("act", (lambda bg=bg, sb_=sb_: A_.activation(out=sig[sb_], in_=ps[:, bg, :], func=AF.Sigmoid)),
                      reads=[("ps", bg)], writes=[("sig", sb_)])
                T.add("dve", (lambda ba=ba, sb_=sb_, c=c: V.tensor_tensor(out=uT[:, c, 30:542], in0=ps[:, ba, :],
                                                                          in1=sig[sb_], op=ALU.mult)),
                      reads=[("ps", ba), ("sig", sb_)], writes=[("uT", c)])
                pump(8)
            enqueue_pair(p)
        for s in range(8):
            sl, sk = ws_next("k16")
            for cc in range(2):
                h = (2 * s + cc) % 8
                b = next_bank()
                mm_group(ps[:, b, :], [(sl[:, kt, cc * 128:(cc + 1) * 128], hT[:, kt, :]) for kt in range(16)],
                         reads=[sk] + hT_keys, writes=[("ps", b)])
                if s < 4:
                    dst, dk = qT[:, h, :], ("qT", h)
                else:
                    dst, dk = kT[:, h, T0:T0 + 512], ("kT", h, tg)
                if cc == 0:
                    T.add("act", (lambda b=b, dst=dst: A_.copy(out=dst, in_=ps[:, b, :])),
                          reads=[("ps", b)], writes=[dk])
                else:
                    T.add("dve", (lambda b=b, dst=dst: V.tensor_copy(out=dst, in_=ps[:, b, :])),
                          reads=[("ps", b)], writes=[dk])
                pump(4)
        def build_diag(c, db):
            w31c = cst[:, C_W31 + c * 31:C_W31 + (c + 1) * 31]
            T.add("dve", (lambda db=db, w31c=w31c: V.tensor_tensor(
                out=diag[db], in0=ident[:].unsqueeze(1).to_broadcast([128, 31, 128]),
                in1=w31c.unsqueeze(2).to_broadcast([128, 31, 128]), op=ALU.mult)),
                reads=["ident", "cst"], writes=[f"diag{db}"])
        if n_pe > 0:
            T.inherit(["diag0"], [("attnT", t) for t in range(4)])
            build_diag(8 - n_pe, 0)
        for s in range(4):
            sl, sk = ws_next("k16")
            for tt in range(4):
                b = next_bank()
                mm_group(ps[:, b, 0:256], [(hT[:, kt, tt * 128:(tt + 1) * 128], sl[:, kt, :]) for kt in range(16)],
                         reads=[sk, ("hT", tt)], writes=[("ps", b)])
                dst = Vaug[:, tg * 4 + tt, 2 * s:2 * s + 2, 0:128]
                src = ps[:, b, 0:256].rearrange("p (h d) -> p h d", d=128)
                if tt % 2 == 0:
                    T.add("act", (lambda dst=dst, src=src: A_.copy(out=dst, in_=src)),
                          reads=[("ps", b)], writes=[("V", tg * 4 + tt)])
                else:
                    T.add("dve", (lambda dst=dst, src=src: V.tensor_copy(out=dst, in_=src)),
                          reads=[("ps", b)], writes=[("V", tg * 4 + tt)])
                pump(1)
        if tg == 0:
            dbg_dump("uT", uT[:, :, 30:542], [("uT", c_) for c_ in range(8)])
            dbg_dump("qT", qT, [("qT", h) for h in range(8)])
            dbg_dump("kT", kT[:, :, 0:512], [("kT", h, 0) for h in range(8)])
            dbg_dump("V", Vaug[:, 0:4, :, 0:128], [("V", t) for t in range(4)])

        T.inherit(["diag1"], hT_keys)
        for ci_, c in enumerate(range(8 - n_pe, 8)):
            db = ci_ % 2
            if ci_ > 0:
                build_diag(c, db)
            b = next_bank(4)
            mm_group(ps[:, b, :], [(diag[db][:, j, :], uT[:, c, j:j + 512]) for j in range(31)],
                     reads=[f"diag{db}", ("uT", c), "uT_halo"], writes=[("ps", b)])
            T.add("act", (lambda b=b, c=c: A_.activation(out=ybf[:, c, :], in_=ps[:, b, :], func=AF.Identity,
                                                         bias=col(C_CB + c))),
                  reads=[("ps", b), "cst"], writes=[("ybf", c)])
            T.add("act", (lambda b=b, c=c: A_.activation(out=ysq_all[:, c, :], in_=ps[:, b, :], func=AF.Square,
                                                         bias=col(C_CB + c))),
                  reads=[("ps", b), "cst"], writes=[("ysq", c)])
            pump(2)

        T.inherit([("PT", b) for b in range(4)] + ["attn_tok"], hT_keys + ["diag0", "diag1"])
        T.inherit([("osb", i) for i in range(4)] + ["t0b", "Osb"], ["gfin", ("cbuf", 0), ("cbuf", 1), ("cbuf", 2)])
        nj = 4 * tg + 4
        steps = [(h, j) for h in range(8) for j in range(nj)]

        def emit_ST(k):
            h, j = steps[k]
            pb = k % 2
            c0 = max(0, j - 4 * tg) * 128

            def fn():
                i0 = PE.matmul(ps[:, 2 * pb, c0:512], kT[0:64, h, j * 128:(j + 1) * 128], qT[0:64, h, c0:512],
                               start=True, stop=True)
                i1 = PE.matmul(ps[:, 2 * pb + 1, c0:512], kT[64:128, h, j * 128:(j + 1) * 128],
                               qT[64:128, h, c0:512], start=True, stop=True)
                return i0, i1
            T.add("pe", fn, reads=[("kT", h, j // 4), ("qT", h)], writes=[("ps", 2 * pb), ("ps", 2 * pb + 1)])

        emit_ST(0)
        deferred = [None]
        for k, (h, j) in enumerate(steps):
            if k + 1 < len(steps):
                emit_ST(k + 1)
            pb = k % 2
            ptb = k % 4
            r = j - 4 * tg
            c0 = max(0, r) * 128
            T.add("act", (lambda pb=pb, ptb=ptb, c0=c0: A_.activation(out=PT[ptb][:, :, c0:512],
                                                                       in_=ps[:, 2 * pb:2 * pb + 2, c0:512],
                                                                       func=AF.Exp, scale=0.125)),
                  reads=[("ps", 2 * pb), ("ps", 2 * pb + 1)], writes=[("PT", ptb)])
            if r >= 0:
                T.add("pool", (lambda ptb=ptb, c0=c0: P_.memset(PT[ptb][64:128, :, c0:c0 + 64], 0.0)),
                      writes=[("PT", ptb)])

            if deferred[0] is not None:
                deferred[0]()
                deferred[0] = None

            def pv(h=h, j=j, ptb=ptb, r=r):
                first = last = None
                for il in range(max(0, r), 4):
                    for m in range(2):
                        bank = 4 + 2 * m + il // 2
                        o_ap = ps[:, bank, (il % 2) * 129:(il % 2) * 129 + 129]
                        ins = PE.matmul(o_ap, PT[ptb][:, m, il * 128:(il + 1) * 128], Vaug[:, j, h, :],
                                        start=(j == 0 and il % 2 == 0), stop=(j == 4 * tg + il),
                                        skip_group_check=True)
                        if first is None:
                            first = ins
                        last = ins
                return first, last
            T.add("pe", pv, reads=[("PT", ptb), ("V", j)], writes=[("ps", 4), ("ps", 5), ("ps", 6), ("ps", 7)])
            if j < nj - 2:
                pump(1)

            def oproc(h=h):
                okeys = [("ps", 4), ("ps", 5), ("ps", 6), ("ps", 7)]
                if tg < 2:
                    T.add("act", lambda: A_.copy(out=Osb, in_=ps[:, 4:8, 0:258]), reads=okeys, writes=["Osb"])
                else:
                    T.add("dve", lambda: V.tensor_copy(out=Osb, in_=ps[:, 4:8, 0:258]), reads=okeys, writes=["Osb"])
                Ov = Osb.rearrange("p b (s c) -> p b s c", c=129)
                O0 = Ov[:, 0:2, :, 0:128]
                O1 = Ov[:, 2:4, :, 0:128]
                osb4 = osb.rearrange("p (a s) d -> p a s d", s=2)
                t04 = t0b4.rearrange("p (a s) d -> p a s d", s=2)
                rec3 = rec[:].rearrange("p (b s) -> p b s", s=2)
                T.add("dve", lambda: V.reciprocal(out=rec3, in_=Ov[:, :, :, 128]), reads=["Osb"], writes=["rec"])
                T.add("dve", lambda: V.tensor_tensor(out=t04, in0=O0, in1=rec3[:, 0:2, :].unsqueeze(3).to_broadcast([128, 2, 2, 128]),
                                                     op=ALU.mult),
                      reads=["Osb", "rec"], writes=["t0b"])
                T.add("dve", lambda: V.tensor_tensor(out=osb4, in0=O1, in1=rec3[:, 2:4, :].unsqueeze(3).to_broadcast([128, 2, 2, 128]),
                                                     op=ALU.mult),
                      reads=["Osb", "rec"], writes=[("osb", i) for i in range(4)])
                T.add("dve", lambda: V.scalar_tensor_tensor(out=osb, in0=osb, scalar=lamt[:, 5:6], in1=t0b4,
                                                            op0=ALU.mult, op1=ALU.add),
                      reads=[("osb", i) for i in range(4)] + ["t0b", "neglam"], writes=[("osb", i) for i in range(4)])
                if tg < 2:
                    def sqf():
                        first = last = None
                        for il in range(4):
                            ins = A_.activation(out=t0b4[:, il, :], in_=osb[:, il, :], func=AF.Square,
                                                accum_out=st[:, 40 + il:41 + il])
                            if first is None:
                                first = ins
                            last = ins
                        return first, last
                    T.add("act", sqf, reads=[("osb", i) for i in range(4)], writes=["t0b", ("st", 40)])
                else:
                    T.add("dve", lambda: V.tensor_tensor(out=t0b4, in0=osb, in1=osb, op=ALU.mult),
                          reads=[("osb", i) for i in range(4)], writes=["t0b"])
                    T.add("dve", lambda: V.reduce_sum(out=st[:, 40:44], in_=t0b4, axis=mybir.AxisListType.X),
                          reads=["t0b"], writes=[("st", 40)])
                T.add("dve", lambda: V.tensor_scalar(out=st[:, 44:48], in0=st[:, 40:44], scalar1=1.0 / 128,
                                                     scalar2=NORM_EPS, op0=ALU.mult, op1=ALU.add),
                      reads=[("st", 40)], writes=[("st", 44)])
                T.add("pool", lambda: P_.tensor_tensor(out=st[:, 48:52], in0=st[:, 44:48], in1=mhalf[:, 0:4], op=ALU.pow),
                      reads=[("st", 44), "mhalf"], writes=[("st", 48)])
                T.add("dve", (lambda h=h: V.tensor_tensor(out=attn_tok[:, :, h * 128:(h + 1) * 128], in0=osb,
                                                          in1=st[:, 48:52].unsqueeze(2).to_broadcast([128, 4, 128]),
                                                          op=ALU.mult)),
                      reads=[("osb", i) for i in range(4)] + [("st", 48)], writes=["attn_tok"])
                if tg == 0 and h == 0:
                    dbg_dump("rec", rec[:], ["rec"])
                    dbg_dump("osb", osb, [("osb", i) for i in range(4)])
                    dbg_dump("st", st[:], [("st", 48), ("st", 44)])
                    dbg_dump("lamt", lamt[:], ["neglam"])
            if j == nj - 1:
                deferred[0] = oproc
        if deferred[0] is not None:
            deferred[0]()
            deferred[0] = None
        if tg == 0:
            dbg_dump("attn_tok", attn_tok, ["attn_tok"], view=lambda d: d)
        T.inherit([("attnT", t) for t in range(4)], ["diag0"])
        for il in range(4):
            def tr(il=il):
                first = last = None
                for c in range(8):
                    ins = PE.transpose(psT8[:, c, :], attn_tok[:, il, c * 128:(c + 1) * 128], ident[:])
                    if first is None:
                        first = ins
                    last = ins
                return first, last
            T.add("pe", tr, reads=["attn_tok", "ident"], writes=[("ps", 0)])
            T.add("dve", (lambda il=il: V.tensor_scalar(out=attnT[:, :, il * 128:(il + 1) * 128], in0=psT8, scalar1=gs[:, 0:1],
                                                           scalar2=None, op0=ALU.mult)),
                  reads=[("ps", 0), "gs"], writes=[("attnT", il)])

        pump(10 ** 6)
        for c in range(8):
            T.add("pe", (lambda c=c: PE.matmul(ps[:, 4, :], ones[:], ybf[:, c, :], start=(c == 0), stop=(c == 7))),
                  reads=[("ybf", c), "ones"], writes=[("ps", 4)])
        for c in range(8):
            T.add("pe", (lambda c=c: PE.matmul(ps[:, 5, :], ones[:], ysq_all[:, c, :], start=(c == 0), stop=(c == 7))),
                  reads=[("ysq", c), "ones"], writes=[("ps", 5)])
        T.inherit(STK, SQK)
        T.add("pool", lambda: P_.tensor_copy(out=halo_save[:], in_=uT[:, :, 512:542]),
              reads=[("uT", c_) for c_ in range(8)], writes=["halo_save"])
        mean, ex2, rstd_t = stat[0], stat[1], stat[2]
        T.add("dve", lambda: V.tensor_scalar(out=mean, in0=ps[:, 4, :], scalar1=1.0 / 1024, scalar2=None, op0=ALU.mult),
              reads=[("ps", 4)], writes=[("stat", 0)])
        T.add("dve", lambda: V.tensor_scalar(out=ex2, in0=ps[:, 5, :], scalar1=1.0 / 1024, scalar2=LN_EPS,
                                             op0=ALU.mult, op1=ALU.add),
              reads=[("ps", 5)], writes=[("stat", 1)])
        T.add("dve", lambda: V.tensor_tensor(out=rstd_t, in0=mean, in1=mean, op=ALU.mult),
              reads=[("stat", 0)], writes=[("stat", 2)])
        T.add("dve", lambda: V.tensor_tensor(out=ex2, in0=ex2, in1=rstd_t, op=ALU.subtract),
              reads=[("stat", 1), ("stat", 2)], writes=[("stat", 1)])
        T.add("act", lambda: A_.activation(out=rstd_t, in_=ex2, func=AF.Sqrt),
              reads=[("stat", 1)], writes=[("stat", 2)])
        T.add("dve", lambda: V.reciprocal(out=rstd_t, in_=rstd_t),
              reads=[("stat", 2)], writes=[("stat", 2)])
        for c in range(8):
            zb = cbuf[c % 3]
            zk = ("cbuf", c % 3)
            T.add("dve", (lambda c=c, zb=zb: V.tensor_tensor(out=zb, in0=ybf[:, c, :], in1=mean, op=ALU.subtract)),
                  reads=[("ybf", c), ("stat", 0)], writes=[zk])
            T.add("dve", (lambda zb=zb: V.tensor_tensor(out=zb, in0=zb, in1=rstd_t, op=ALU.mult)),
                  reads=[zk, ("stat", 2)], writes=[zk])
            T.add("act", (lambda c=c, zb=zb: A_.activation(out=uT[:, c, 30:542], in_=zb, func=AF.Silu,
                                                           scale=col(C_LNG + c), bias=col(C_LNB + c))),
                  reads=[zk, "cst", "halo_save"], writes=[("uT", c)])
        if tg == 0:
            dbg_dump("convT", uT[:, :, 30:542], [("uT", c_) for c_ in range(8)])
            dbg_dump("attnT", attnT, [("attnT", t) for t in range(4)])

        UTKS = [("uT", c_) for c_ in range(8)]
        wo = {}

        def wo_attn(s_, tt, sl, sk):
            b = next_bank(6)
            wo[(s_, tt)] = b
            mm_group(ps[:, b, 0:256], [(attnT[:, c, tt * 128:(tt + 1) * 128], sl[:, 8 + c, :]) for c in range(8)],
                     reads=[sk, ("attnT", tt)], writes=[("ps", b)], stop_last=False)

        def wo_conv(s_, tt, sl, sk):
            b = wo[(s_, tt)]

            def fn():
                first = last = None
                for c in range(8):
                    ins = PE.matmul(ps[:, b, 0:256], uT[:, c, 30 + tt * 128:30 + (tt + 1) * 128], sl[:, c, :],
                                    start=False, stop=(c == 7))
                    if first is None:
                        first = ins
                    last = ins
                return first, last
            T.add("pe", fn, reads=[sk] + UTKS, writes=[("ps", b)])
            T.add("dve", (lambda: V.tensor_tensor(out=xres[:, tt, s_ * 256:(s_ + 1) * 256], in0=ps[:, b, 0:256],
                                                  in1=xres[:, tt, s_ * 256:(s_ + 1) * 256], op=ALU.add)),
                  reads=[("ps", b), ("xres", tt)], writes=[("xres", tt)])
        sl0, sk0 = ws_next("k16")
        sl1, sk1 = ws_next("k16", held=1)
        for tt in range(4):
            wo_attn(0, tt, sl0, sk0)
        for tt in range(2):
            wo_attn(1, tt, sl1, sk1)
        for tt in range(4):
            wo_conv(0, tt, sl0, sk0)
        for tt in range(2):
            wo_conv(1, tt, sl1, sk1)
        for tt in range(2, 4):
            wo_attn(1, tt, sl1, sk1)
            wo_conv(1, tt, sl1, sk1)
        for s in range(2, 8):
            sl, sk = ws_next("k16")
            for tt in range(4):
                wo_attn(s, tt, sl, sk)
                wo_conv(s, tt, sl, sk)
        if tg == 0:
            dbg_dump("x1", xres[:, 0, :], [("xres", 0)])

        T.inherit([("h2T", t) for t in range(4)], hT_keys + ["diag0", "diag1"] + [("PT", b) for b in range(4)] + ["attn_tok"])
        T.inherit(HBK, YK + SQK + STK + YOK)
        for tt in range(4):
            norm_transpose(xres[:, tt, :], ("xres", tt), tt, C_GFFN, hT, lambda t: ("h2T", t))
        h2_keys = [("h2T", t) for t in range(4)]

        T.inherit(["act"], [("qT", h) for h in range(8)] + [("uT", c_) for c_ in range(8)] + ["uT_halo"] + [("attnT", t) for t in range(4)] + HNK)
        T.inherit(["gfin", ("cbuf", 0), ("cbuf", 1), ("cbuf", 2)], [("osb", i) for i in range(4)] + ["t0b", "Osb"])
        T.add("sp", lambda: SP.dma_start(out=gfin, in_=gfin_d[:, :]), writes=["gfin"], dma=True)
        for hh in range(2):
            for p in range(11):
                Gsl, Gk = ws_next("k16")
                Usl, Uk = ws_next("k16", held=1)
                for cc in range(2):
                    fl = 2 * p + cc
                    fc = hh * 22 + fl
                    pr = next_bank(4)
                    bg, bu = 2 * pr, 2 * pr + 1
                    bank_rr[0] += 0
                    mm_group(ps[:, bg, :], [(Gsl[:, kt, cc * 128:(cc + 1) * 128], hT[:, kt, :]) for kt in range(16)],
                             reads=[Gk] + h2_keys, writes=[("ps", bg)])
                    mm_group(ps[:, bu, :], [(Usl[:, kt, cc * 128:(cc + 1) * 128], hT[:, kt, :]) for kt in range(16)],
                             reads=[Uk] + h2_keys, writes=[("ps", bu)])
                    ci = fc % 3
                    cb_, ck = cbuf[ci], ("cbuf", ci)
                    w0, w1, w2 = col(C_FW + fc * 3), col(C_FW + fc * 3 + 1), col(C_FW + fc * 3 + 2)
                    T.add("act", (lambda bg=bg, cb_=cb_, w2=w2, fc=fc: A_.activation(out=cb_, in_=ps[:, bg, :],
                                                                                     func=AF.Identity, scale=w2,
                                                                                     bias=col(C_FB + fc))),
                          reads=[("ps", bg), "cst"], writes=[ck])
                    T.add("dve", (lambda bg=bg, cb_=cb_, w1=w1: V.scalar_tensor_tensor(out=cb_[:, 1:512],
                                                                                       in0=ps[:, bg, 0:511], scalar=w1,
                                                                                       in1=cb_[:, 1:512],
                                                                                       op0=ALU.mult, op1=ALU.add)),
                          reads=[("ps", bg), ck, "cst"], writes=[ck])
                    T.add("dve", (lambda bg=bg, cb_=cb_, w0=w0: V.scalar_tensor_tensor(out=cb_[:, 2:512],
                                                                                       in0=ps[:, bg, 0:510], scalar=w0,
                                                                                       in1=cb_[:, 2:512],
                                                                                       op0=ALU.mult, op1=ALU.add)),
                          reads=[("ps", bg), ck, "cst"], writes=[ck])
                    T.add("dve", (lambda cb_=cb_, w0=w0, fc=fc: V.scalar_tensor_tensor(out=cb_[:, 0:2],
                                                                                       in0=fhalo[:, fc, 0:2], scalar=w0,
                                                                                       in1=cb_[:, 0:2],
                                                                                       op0=ALU.mult, op1=ALU.add)),
                          reads=["fhalo", ck, "cst"], writes=[ck])
                    T.add("dve", (lambda cb_=cb_, w1=w1, fc=fc: V.scalar_tensor_tensor(out=cb_[:, 0:1],
                                                                                       in0=fhalo[:, fc, 1:2], scalar=w1,
                                                                                       in1=cb_[:, 0:1],
                                                                                       op0=ALU.mult, op1=ALU.add)),
                          reads=["fhalo", ck, "cst"], writes=[ck])
                    T.add("dve", (lambda bg=bg, fc=fc: V.tensor_copy(out=fhalo[:, fc, :], in_=ps[:, bg, 510:512])),
                          reads=[("ps", bg)], writes=["fhalo"])
                    T.add("act", (lambda cb_=cb_: A_.activation(out=cb_, in_=cb_, func=AF.Gelu)),
                          reads=[ck], writes=[ck])
                    T.add("dve", (lambda cb_=cb_, bu=bu, fl=fl: V.tensor_tensor(out=act[:, fl, :], in0=ps[:, bu, :],
                                                                                in1=cb_, op=ALU.mult)),
                          reads=[("ps", bu), ck], writes=["act"])
            if tg == 0 and hh == 0:
                dbg_dump("act", act[:, 0:4, :], ["act"])
            for dq in range(4):
                base = (dq % 2) * 4
                bks = [("ps", base + tt) for tt in range(4)]
                for (f0, nf) in ((0, 8), (8, 8), (16, 6)):
                    sl, sk = ws_next("f")

                    def dn(sl=sl, f0=f0, nf=nf, base=base):
                        first = last = None
                        for fi in range(nf):
                            fl = f0 + fi
                            for tt in range(4):
                                ins = PE.matmul(ps[:, base + tt, :], act[:, fl, tt * 128:(tt + 1) * 128], sl[:, fi, :],
                                                start=(fl == 0), stop=(fl == 21))
                                if first is None:
                                    first = ins
                                last = ins
                        return first, last
                    T.add("pe", dn, reads=[sk, "act"], writes=bks)
                for tt in range(4):
                    T.add("dve", (lambda tt=tt, dq=dq, base=base: V.tensor_tensor(
                        out=xres[:, tt, dq * 512:(dq + 1) * 512], in0=ps[:, base + tt, :],
                        in1=xres[:, tt, dq * 512:(dq + 1) * 512], op=ALU.add)),
                        reads=[("ps", base + tt), ("xres", tt)], writes=[("xres", tt)])

        T.inherit(YOK, HBK + YK + SQK + STK)
        for tt in range(4):
            b = tt % 2
            c0 = 8 * tt
            rms_stats(xres[:, tt, :], ("xres", tt), yjunk[b], ("yout", b), c0, 1.0 / D, NORM_EPS)
            T.add("dve", (lambda tt=tt, c0=c0, b=b: V.scalar_tensor_tensor(out=yout[b], in0=xres[:, tt, :],
                                                                           scalar=st[:, c0 + 2:c0 + 3], in1=gfin,
                                                                           op0=ALU.mult, op1=ALU.mult)),
                  reads=[("xres", tt), ("st", c0 + 2), "gfin"], writes=[("yout", b)])
            r0 = T0 + tt * 128
            T.add("sp", (lambda b=b, r0=r0: SP.dma_start(out=out_d[r0:r0 + 128, :], in_=yout[b])),
                  reads=[("yout", b)], dma=True)
            if tg + 1 < NG:
                T.add("sp", (lambda tt=tt, r1=r0 + 512: SP.dma_start(out=xres[:, tt, :], in_=x_d[r1:r1 + 128, :])),
                      writes=[("xres", tt)], dma=True)

    assert ws_state["used"] == len(slabs), (ws_state, len(slabs))
    nw = T.emit()
    for q, sems in T.dsem.items():
        for sl, sem in enumerate(sems):
            v = T.dlast_val[q][sl]
            if v > 0:
                nc.sync.wait_ge(sem, v)
    nc.all_engine_barrier()
    allsems = list(T.esem.values()) + [sm for sems in T.dsem.values() for sm in sems]
    nc.clear_and_free_semaphores(allsems)
    nc.all_engine_barrier()
    return nc, T, nw


_CACHE = {}


def _prep_consts(inp):
    c = np.zeros((128, C_TOT), np.float32)
    c[:, C_GMIX:C_GMIX + 16] = inp["norm_mix_g"][0].reshape(16, 128).T
    c[:, C_GFFN:C_GFFN + 16] = inp["norm_ffn_g"][0].reshape(16, 128).T
    w31 = inp["conv_dw_w"][0].reshape(31, 8, 128).transpose(2, 1, 0)
    c[:, C_W31:C_W31 + 248] = w31.reshape(128, 248)
    c[:, C_CB:C_CB + 8] = inp["conv_dw_b"][0].reshape(8, 128).T
    c[:, C_LNG:C_LNG + 8] = inp["conv_ln_g"][0].reshape(8, 128).T
    c[:, C_LNB:C_LNB + 8] = inp["conv_ln_b"][0].reshape(8, 128).T
    fw = inp["ffn_dw_w"][0].reshape(3, NFC, 128).transpose(2, 1, 0)
    c[:, C_FW:C_FW + 132] = fw.reshape(128, 132)
    c[:, C_FB:C_FB + NFC] = inp["ffn_dw_b"][0].reshape(NFC, 128).T
    c[:, C_LQ1:C_LQ1 + 64] = np.broadcast_to(inp["lambda_q1"][0], (128, 64))
    c[:, C_LK1:C_LK1 + 64] = np.broadcast_to(inp["lambda_k1"][0], (128, 64))
    c[:, C_LQ2:C_LQ2 + 64] = np.broadcast_to(inp["lambda_q2"][0], (128, 64))
    c[:, C_LK2:C_LK2 + 64] = np.broadcast_to(inp["lambda_k2"][0], (128, 64))
    c[:, C_SUB] = inp["subln_g"][0]
    return c


def kernel(**inputs):
    inp = {k: np.asarray(v, dtype=np.float32) for k, v in inputs.items()}
    debug = _CACHE.get("debug")
    nc = build_full(debug)
    consts = _prep_consts(inp)
    gfin = np.ascontiguousarray(np.broadcast_to(inp["norm_final_g"].reshape(1, D), (128, D)))
    shared = {
        "w_in": np.ascontiguousarray(inp["w_in"][0]),
        "w_out": np.ascontiguousarray(inp["w_out"][0]),
        "w_up": np.ascontiguousarray(inp["w_up"][0]),
        "w_down": np.ascontiguousarray(inp["w_down"][0]),
        "consts": consts,
        "gfin": gfin,
    }
    ncores = _CACHE.get("ncores", NCORES)
    in_maps = []
    for c in range(ncores):
        m = dict(shared)
        m["x"] = np.ascontiguousarray(inp["x"][c])
        in_maps.append(m)
    res = run_bass_kernel_spmd(nc, in_maps, core_ids=list(range(ncores)))
    _CACHE["last_results"] = res.results
    out = np.stack([np.asarray(r["out"], dtype=np.float32).reshape(SEQ, D) for r in res.results], axis=0)
    if ncores < NCORES:
        full = np.zeros((NCORES, SEQ, D), np.float32)
        full[:ncores] = out
        out = full
    return out


def build_full(debug=None):
    nc, T, nw = build(debug)
    return nc
```
